# Optimizing a Trainium2 kernel written in Bass

```python
import jax, jax.numpy as jnp
from jax import lax

D_MODEL = 1024
BATCH = 8
SEQ = 2048
DEPTH = 4

GRID_W = 64
CTX_LEN = 256
NORM_EPS = 1e-6
N_MOD = 6
MASK_VALUE = -1e30

POOL_WINDOWS = (2, 4, 8, 16)
N_POOL_GROUPS = len(POOL_WINDOWS)
POOL_GROUP_DIM = D_MODEL // 8
D_POOL = N_POOL_GROUPS * POOL_GROUP_DIM

HEAD_DIM = 64
N_Q_HEADS = D_MODEL // 128
N_KV_HEADS = N_Q_HEADS // 4
GQA_GROUP = N_Q_HEADS // N_KV_HEADS
D_ATTN = N_Q_HEADS * HEAD_DIM
D_KV = N_KV_HEADS * HEAD_DIM
WINDOW = 128
ATTN_BLOCK = 128
ROPE_THETA = 10000.0
ROPE_AXIS_DIM = HEAD_DIM // 2
ROPE_FREQS = ROPE_AXIS_DIM // 2

CHUNK = 128
N_SG_GROUPS = 4
D_SG = D_MODEL // 2
SG_GROUP_DIM = D_SG // N_SG_GROUPS

N_BRANCHES = 3
OFF_Q = D_POOL
OFF_K = OFF_Q + D_ATTN
OFF_V = OFF_K + D_KV
OFF_U = OFF_V + D_KV
OFF_SV = OFF_U + D_SG
OFF_GATE = OFF_SV + D_SG
D_IN = OFF_GATE + N_BRANCHES * D_MODEL

D_FF = -(-8 * D_MODEL // (3 * 256)) * 256

kernel_name = 'hybrid_pool_swa_gmlp_dit'


def rmsnorm(x, gain):
    xf = x.astype(jnp.float32)
    y = xf * lax.rsqrt(jnp.mean(xf * xf, axis=-1, keepdims=True) + NORM_EPS)
    return (y * gain.astype(jnp.float32)).astype(x.dtype)


def modulate(x, gain, shift, scale):
    return rmsnorm(x, gain) * (1 + scale) + shift


def heads(t, n):
    return t.reshape(t.shape[:-1] + (n, HEAD_DIM))


def grid_rope_tables(rows):
    row = jnp.repeat(jnp.arange(rows), GRID_W).astype(jnp.float32)
    col = jnp.tile(jnp.arange(GRID_W), rows).astype(jnp.float32)
    inv_freq = ROPE_THETA ** (-jnp.arange(ROPE_FREQS, dtype=jnp.float32) / ROPE_FREQS)
    ang_r = row[:, None] * inv_freq[None, :]
    ang_c = col[:, None] * inv_freq[None, :]
    return (jnp.cos(ang_r), jnp.sin(ang_r), jnp.cos(ang_c), jnp.sin(ang_c))


def _rotate(xp, cos, sin):
    x1, x2 = xp[..., :ROPE_FREQS], xp[..., ROPE_FREQS:]
    cos = cos[None, :, None, :]
    sin = sin[None, :, None, :]
    return jnp.concatenate([x1 * cos - x2 * sin, x2 * cos + x1 * sin], axis=-1)


def rope_2d(t, tables):
    cos_r, sin_r, cos_c, sin_c = tables
    tf = t.astype(jnp.float32)
    out = jnp.concatenate([_rotate(tf[..., :ROPE_AXIS_DIM], cos_r, sin_r),
                           _rotate(tf[..., ROPE_AXIS_DIM:], cos_c, sin_c)], axis=-1)
    return out.astype(t.dtype)


def multiscale_pool(a, w_pool, pool_scale):
    bsz, length = a.shape[:2]
    af = a.astype(jnp.float32)
    csum = jnp.concatenate([jnp.zeros_like(af[:, :1]), jnp.cumsum(af, axis=1)], axis=1)
    pos = jnp.arange(length)
    means = []
    for g, w in enumerate(POOL_WINDOWS):
        lo = jnp.clip(pos - w // 2, 0, length)
        hi = jnp.clip(pos + (w - w // 2), 0, length)
        cg = csum[..., g * POOL_GROUP_DIM:(g + 1) * POOL_GROUP_DIM]
        cnt = (hi - lo).astype(jnp.float32)[:, None]
        means.append((cg[:, hi] - cg[:, lo]) / cnt)
    pooled = jnp.stack(means, axis=2) - af.reshape(bsz, length, N_POOL_GROUPS, POOL_GROUP_DIM)
    mixed = jnp.einsum('blgc,gcd->blgd', pooled.astype(a.dtype), w_pool)
    return mixed.reshape(bsz, length, D_POOL) * pool_scale


def banded(t, n_blocks):
    bsz = t.shape[0]
    tp = jnp.pad(t, ((0, 0), (ATTN_BLOCK, ATTN_BLOCK), (0, 0), (0, 0)))
    tb = tp.reshape(bsz, n_blocks + 2, ATTN_BLOCK, N_KV_HEADS, HEAD_DIM)
    return jnp.concatenate([tb[:, :-2], tb[:, 1:-1], tb[:, 2:]], axis=2)


def sink_column(sink, shape):
    s = sink.astype(jnp.float32).reshape(N_KV_HEADS, GQA_GROUP, 1, 1)
    return jnp.broadcast_to(s, shape[:-1] + (1,))


def window_attention_with_context(q, k, v, k_ctx, v_ctx, sink):
    bsz, length = q.shape[:2]
    nb = length // ATTN_BLOCK
    scale = HEAD_DIM ** -0.5
    qb = q.reshape(bsz, nb, ATTN_BLOCK, N_KV_HEADS, GQA_GROUP, HEAD_DIM)
    kb, vb = banded(k, nb), banded(v, nb)
    s_loc = jnp.einsum('bnqhgd,bnkhd->bnhgqk', qb, kb, preferred_element_type=jnp.float32) * scale
    q_pos = jnp.arange(nb)[:, None, None] * ATTN_BLOCK + jnp.arange(ATTN_BLOCK)[None, :, None]
    k_pos = jnp.arange(nb)[:, None, None] * ATTN_BLOCK - ATTN_BLOCK + jnp.arange(3 * ATTN_BLOCK)[None, None, :]
    valid = (jnp.abs(q_pos - k_pos) <= WINDOW) & (k_pos >= 0) & (k_pos < length)
    s_loc = jnp.where(valid[None, :, None, None], s_loc, MASK_VALUE)
    s_ctx = jnp.einsum('bnqhgd,bchd->bnhgqc', qb, k_ctx, preferred_element_type=jnp.float32) * scale
    logits = jnp.concatenate([s_loc, s_ctx, sink_column(sink, s_loc.shape)], axis=-1)
    p = jax.nn.softmax(logits, axis=-1).astype(v.dtype)
    n_loc = 3 * ATTN_BLOCK
    n_ctx = k_ctx.shape[1]
    out = (jnp.einsum('bnhgqk,bnkhd->bnqhgd', p[..., :n_loc], vb)
           + jnp.einsum('bnhgqc,bchd->bnqhgd', p[..., n_loc:n_loc + n_ctx], v_ctx))
    return out.reshape(bsz, length, D_ATTN)


def context_self_attention(q, k, v, sink):
    bsz, n = q.shape[:2]
    scale = HEAD_DIM ** -0.5
    qg = q.reshape(bsz, n, N_KV_HEADS, GQA_GROUP, HEAD_DIM)
    s = jnp.einsum('bqhgd,bkhd->bhgqk', qg, k, preferred_element_type=jnp.float32) * scale
    p = jax.nn.softmax(jnp.concatenate([s, sink_column(sink, s.shape)], axis=-1), axis=-1)
    out = jnp.einsum('bhgqk,bkhd->bqhgd', p[..., :n].astype(v.dtype), v)
    return out.reshape(bsz, n, D_ATTN)


def spatial_gating(u, v, v_gain, w_spatial, b_spatial):
    bsz, length = u.shape[:2]
    nc = length // CHUNK
    vn = rmsnorm(v, v_gain).reshape(bsz, nc, CHUNK, N_SG_GROUPS, SG_GROUP_DIM)
    mixed = jnp.einsum('gpr,bnrgc->bnpgc', w_spatial, vn) + b_spatial.T[:, :, None]
    return u * mixed.reshape(bsz, length, D_SG)


def mixer_merge(z, y_attn, w_pool, pool_scale, sg_v_gain, w_spatial, b_spatial,
                w_br_pool, w_br_attn, w_br_sg, w_out):
    y_pool = multiscale_pool(z[..., :OFF_Q], w_pool, pool_scale)
    y_sg = spatial_gating(jax.nn.gelu(z[..., OFF_U:OFF_SV]), jax.nn.gelu(z[..., OFF_SV:OFF_GATE]),
                          sg_v_gain, w_spatial, b_spatial)
    g = jax.nn.sigmoid(z[..., OFF_GATE:].reshape(z.shape[:-1] + (N_BRANCHES, D_MODEL)))
    y = (g[..., 0, :] * (y_pool @ w_br_pool)
         + g[..., 1, :] * (y_attn @ w_br_attn)
         + g[..., 2, :] * (y_sg @ w_br_sg))
    return y @ w_out


def swiglu(h, w_in, w_out):
    gu = h @ w_in
    return (jax.nn.silu(gu[..., :D_FF]) * gu[..., D_FF:]) @ w_out


def setup_inputs(seed: int = 0) -> dict:
    key = jax.random.key(seed)
    ks = jax.random.split(key, 24)
    f32 = jnp.float32

    def nrm(k, shape, scale):
        return jax.random.normal(k, shape, f32) * scale

    def gain(k, shape):
        return 1.0 + 0.02 * jax.random.normal(k, shape, f32)

    return {
        'x': nrm(ks[0], (BATCH, SEQ, D_MODEL), 1.0),
        'c': nrm(ks[1], (BATCH, D_MODEL), 1.0),
        'ctx': nrm(ks[2], (BATCH, CTX_LEN, D_MODEL), 1.0),
        'c_ctx': nrm(ks[3], (D_MODEL,), 1.0),
        'w_mod': nrm(ks[4], (DEPTH, D_MODEL, N_MOD * D_MODEL), 0.5 * D_MODEL ** -0.5),
        'b_mod': nrm(ks[5], (DEPTH, N_MOD * D_MODEL), 0.02),
        'norm1_gain': gain(ks[6], (DEPTH, D_MODEL)),
        'norm2_gain': gain(ks[7], (DEPTH, D_MODEL)),
        'w_in': nrm(ks[8], (DEPTH, D_MODEL, D_IN), D_MODEL ** -0.5),
        'w_pool': nrm(ks[9], (DEPTH, N_POOL_GROUPS, POOL_GROUP_DIM, POOL_GROUP_DIM), POOL_GROUP_DIM ** -0.5),
        'pool_scale': gain(ks[10], (DEPTH, D_POOL)),
        'attn_sink': nrm(ks[11], (DEPTH, N_Q_HEADS), 0.5),
        'sg_v_gain': gain(ks[12], (DEPTH, D_SG)),
        'w_spatial': nrm(ks[13], (DEPTH, N_SG_GROUPS, CHUNK, CHUNK), CHUNK ** -0.5),
        'b_spatial': gain(ks[14], (DEPTH, N_SG_GROUPS, CHUNK)),
        'w_br_pool': nrm(ks[15], (DEPTH, D_POOL, D_MODEL), D_POOL ** -0.5),
        'w_br_attn': nrm(ks[16], (DEPTH, D_ATTN, D_MODEL), D_ATTN ** -0.5),
        'w_br_sg': nrm(ks[17], (DEPTH, D_SG, D_MODEL), D_SG ** -0.5),
        'w_out': nrm(ks[18], (DEPTH, D_MODEL, D_MODEL), D_MODEL ** -0.5),
        'w_ffn_in': nrm(ks[19], (DEPTH, D_MODEL, 2 * D_FF), D_MODEL ** -0.5),
        'w_ffn_out': nrm(ks[20], (DEPTH, D_FF, D_MODEL), D_FF ** -0.5),
        'final_gain': gain(ks[21], (D_MODEL,)),
    }


def reference(x, c, ctx, c_ctx, w_mod, b_mod, norm1_gain, norm2_gain, w_in, w_pool, pool_scale,
              attn_sink, sg_v_gain, w_spatial, b_spatial, w_br_pool, w_br_attn, w_br_sg, w_out,
              w_ffn_in, w_ffn_out, final_gain):
    rows = x.shape[1] // GRID_W
    rope = grid_rope_tables(rows)
    cx = ctx
    sc = jax.nn.silu(c)
    scc = jax.nn.silu(c_ctx)
    for i in range(DEPTH):
        last = i == DEPTH - 1
        mod_x = jnp.split((sc @ w_mod[i] + b_mod[i])[:, None, :], N_MOD, axis=-1)
        mod_c = jnp.split((scc @ w_mod[i] + b_mod[i])[None, None, :], N_MOD, axis=-1)
        layer = (w_pool[i], pool_scale[i], sg_v_gain[i], w_spatial[i], b_spatial[i],
                 w_br_pool[i], w_br_attn[i], w_br_sg[i], w_out[i])

        hc = modulate(cx, norm1_gain[i], mod_c[0], mod_c[1])
        if last:
            zc = hc @ w_in[i, :, OFF_K:OFF_U]
            kc, vc = zc[..., :D_KV], zc[..., D_KV:]
        else:
            zc = hc @ w_in[i]
            kc, vc = zc[..., OFF_K:OFF_V], zc[..., OFF_V:OFF_U]
        kc, vc = heads(kc, N_KV_HEADS), heads(vc, N_KV_HEADS)

        hx = modulate(x, norm1_gain[i], mod_x[0], mod_x[1])
        zx = hx @ w_in[i]
        qx = rope_2d(heads(zx[..., OFF_Q:OFF_K], N_Q_HEADS), rope)
        kx = rope_2d(heads(zx[..., OFF_K:OFF_V], N_KV_HEADS), rope)
        vx = heads(zx[..., OFF_V:OFF_U], N_KV_HEADS)
        attn_x = window_attention_with_context(qx, kx, vx, kc, vc, attn_sink[i])
        x = x + mod_x[2] * mixer_merge(zx, attn_x, *layer)
        x = x + mod_x[5] * swiglu(modulate(x, norm2_gain[i], mod_x[3], mod_x[4]), w_ffn_in[i], w_ffn_out[i])

        if not last:
            attn_c = context_self_attention(heads(zc[..., OFF_Q:OFF_K], N_Q_HEADS), kc, vc, attn_sink[i])
            cx = cx + mod_c[2] * mixer_merge(zc, attn_c, *layer)
            cx = cx + mod_c[5] * swiglu(modulate(cx, norm2_gain[i], mod_c[3], mod_c[4]), w_ffn_in[i], w_ffn_out[i])
    return rmsnorm(x, final_gain)
```

```python
import numpy as np
from contextlib import ExitStack
import concourse.bass as bass
import concourse.mybir as mybir
from concourse.bass_utils import run_bass_kernel_spmd

F32 = mybir.dt.float32
BF16 = mybir.dt.bfloat16
AF = mybir.ActivationFunctionType
ALU = mybir.AluOpType

D = 1024; L = 2048; C = 256; NT = L + C; DEPTH = 4; NK = 8
D_FF = 2816
OFF_Q = 512; OFF_K = 1024; OFF_V = 1152; OFF_U = 1280; OFF_SV = 1792; OFF_GATE = 2304
EPS = 1e-6
CH = 2048
NSLOT = 4; PREF = 2
TT = [(0, 512), (512, 512), (1024, 512), (1536, 512), (2048, 256)]
FFG = [list(range(0, 8)), list(range(8, 15)), list(range(15, 22))]
PERM = np.array(list(range(0, 16)) + list(range(32, 48)) + list(range(16, 32)) + list(range(48, 64)))
POOLW = (2, 4, 8, 16)
XPAD = 8
APLEN = XPAD + L + XPAD + XPAD + C + XPAD
CTX_OFF = XPAD + L + XPAD + XPAD
NVEC = 384


def plan_tiles():
    t = []
    for j in range(48): t.append((("mod", j), 8, 128))
    for hk in range(2): t.append((("k", hk), 8, 128))
    for j in range(4): t.append((("q", j), 8, 128))
    t.append((("v",), 8, 128))
    for h in range(2): t.append((("sv", h), 8, 256))
    for g in range(4): t.append((("u", g), 8, 128))
    for g in range(4): t.append((("ws", g), 1, 128))
    for g in range(4):
        t.append((("pc", g), 8, 128)); t.append((("wp", g), 1, 128))
    for j in range(8):
        for b in range(3):
            t.append((("gate", b, j), 8, 128)); t.append((("br", b, j), 4, 128))
    for j in range(8): t.append((("out", j), 8, 128))
    for gi, grp in enumerate(FFG):
        for f in grp:
            t.append((("fg", f), 8, 128)); t.append((("fu", f), 8, 128))
        for j in range(8): t.append((("fo", gi, j), len(grp), 128))
    return t


def pack_plan(tiles):
    pos = []; c = 0; off = 0
    for key, nk, ncol in tiles:
        sz = nk * ncol
        if off + sz > CH:
            c += 1; off = 0
        pos.append((c, off)); off += sz
    return pos, c + 1


TILES = plan_tiles()
TPOS, NCH = pack_plan(TILES)


def tile_data(key, l, W):
    k = key[0]
    if k == "mod": j = key[1]; return W["w_mod"][l][:, j * 128:(j + 1) * 128]
    if k == "k":
        cols = OFF_K + key[1] * 64 + PERM; cols = np.concatenate([cols, cols]); return W["w_in"][l][:, cols]
    if k == "q":
        j = key[1]
        cols = np.concatenate([OFF_Q + (2 * j) * 64 + PERM, OFF_Q + (2 * j + 1) * 64 + PERM]); return W["w_in"][l][:, cols]
    if k == "v": return W["w_in"][l][:, OFF_V:OFF_V + 128]
    if k == "sv": h = key[1]; return W["w_in"][l][:, OFF_SV + h * 256:OFF_SV + (h + 1) * 256]
    if k == "u": g = key[1]; return W["w_in"][l][:, OFF_U + g * 128:OFF_U + (g + 1) * 128]
    if k == "ws": return W["w_spatial"][l, key[1]].T
    if k == "pc": g = key[1]; return W["w_in"][l][:, g * 128:(g + 1) * 128]
    if k == "wp": return W["w_pool"][l, key[1]]
    if k == "gate":
        b, j = key[1], key[2]; o = OFF_GATE + b * 1024 + j * 128; return W["w_in"][l][:, o:o + 128]
    if k == "br":
        b, j = key[1], key[2]; return W[("w_br_pool", "w_br_attn", "w_br_sg")[b]][l][:, j * 128:(j + 1) * 128]
    if k == "out": j = key[1]; return W["w_out"][l][:, j * 128:(j + 1) * 128]
    if k == "fg": f = key[1]; return W["w_ffn_in"][l][:, f * 128:(f + 1) * 128]
    if k == "fu": f = key[1]; return W["w_ffn_in"][l][:, D_FF + f * 128:D_FF + (f + 1) * 128]
    if k == "fo":
        gi, j = key[1], key[2]; grp = FFG[gi]
        return W["w_ffn_out"][l][grp[0] * 128:(grp[-1] + 1) * 128, j * 128:(j + 1) * 128]
    raise KeyError(key)


def pack_layer(l, W):
    out = np.zeros((NCH, 128, CH), np.float32)
    for (key, nk, ncol), (c, off) in zip(TILES, TPOS):
        a = np.asarray(tile_data(key, l, W), dtype=np.float32)
        a = a.reshape(nk, 128, ncol).transpose(1, 0, 2).reshape(128, nk * ncol)
        out[c, :, off:off + nk * ncol] = a
    return out


def const_tables():
    inv_freq = (10000.0 ** (-np.arange(16, dtype=np.float32) / 16)).astype(np.float32)
    t = np.arange(L)
    row = (t // 64).astype(np.float32); col = (t % 64).astype(np.float32)
    ang = np.zeros((32, L), np.float32)
    ang[0:16] = inv_freq[:, None] * row[None, :]
    ang[16:32] = inv_freq[:, None] * col[None, :]
    cos32 = np.cos(ang).astype(np.float32); sin32 = np.sin(ang).astype(np.float32)
    cosT = np.tile(cos32, (4, 1))
    sinS = np.concatenate([sin32, -sin32, sin32, -sin32], axis=0)
    ki = np.arange(128)[:, None]; qi = np.arange(128)[None, :]
    masks = np.stack([(qi <= ki), (ki <= qi)], axis=1).astype(np.float32)
    edge = np.ones((4, 2, 2, 8), np.float32)
    for g, w in enumerate(POOLW):
        h = w // 2
        for e in range(h):
            edge[g, :, 0, e] = 1.0 / (e + h)
        for e in range(h - 1):
            edge[g, :, 1, e] = 1.0 / (h + (h - 1 - e))
    edge = np.broadcast_to(edge.reshape(1, -1), (128, 128)).copy()
    ident = np.eye(128, dtype=np.float32)
    return dict(cosT=cosT, sinS=sinS, masks=masks.reshape(128, 256), edge=edge, ident=ident)


class Buf:
    __slots__ = ("name", "w", "r")

    def __init__(self, name):
        self.name = name; self.w = None; self.r = {}


class Eng:
    def __init__(self, e, sem, name, same=True):
        self.e = e; self.sem = sem; self.cnt = 0; self.seen = {}; self.name = name; self.same = same


class DSem:
    def __init__(self, sem):
        self.sem = sem; self.val = 0


class StopBuild(Exception):
    pass


class Prog:
    def __init__(self, first, last, nlayers, stop_at=None, dump='X'):
        self.first = first; self.last = last; self.nl = nlayers; self.stop_at = stop_at; self.dump = dump
        self.nc = bass.Bass("TRN2", target_bir_lowering=False)
        self.es = ExitStack()
        self.dtot = {}

    def sem(self, name):
        return self.es.enter_context(self.nc.semaphore(name))

    def dsem(self, name):
        d = DSem(self.sem(name)); self.dtot[d.sem.num] = d; return d

    def waits(self, eng, reads, writes):
        need = {}

        def add(ev):
            if ev is None: return
            s, v = ev
            if s.num in self.dtot: v = max(v, self.dtot[s.num].val)
            if need.get(s.num, (None, 0))[1] < v: need[s.num] = (s, v)
        for b in reads: add(b.w)
        for b in writes:
            add(b.w)
            for ev in b.r.values(): add(ev)
        for key, (s, v) in need.items():
            if s is eng.sem:
                if not eng.same or v > eng.cnt: continue
            if eng.seen.get(key, 0) >= v: continue
            eng.e.wait_ge(s, v); eng.seen[key] = v

    def mark(self, ev, reads, writes):
        for b in writes:
            b.w = ev; b.r = {}
        for b in reads:
            b.r[ev[0].num] = ev

    def op(self, eng, fn, reads=(), writes=(), signal=True):
        self.waits(eng, reads, writes)
        ins = fn(eng.e)
        val = eng.cnt + 1
        if signal:
            ins.then_inc(eng.sem, 1); eng.cnt = val
        self.mark((eng.sem, val), reads, writes)
        return ins

    def dma(self, q, out, in_, ds, reads=(), writes=()):
        self.waits(q, reads, writes)
        ds.val += 16
        q.e.dma_start(out=out, in_=in_).then_inc(ds.sem, 16)
        self.mark((ds.sem, ds.val), reads, writes)

    def barrier(self):
        engs = [self.pe, self.act, self.dve, self.pool, self.sp]
        for e in engs:
            for f in engs:
                if f is e or f.sem is None or f.cnt == 0: continue
                if e.seen.get(f.sem.num, 0) >= f.cnt: continue
                e.e.wait_ge(f.sem, f.cnt); e.seen[f.sem.num] = f.cnt
            for d in self.dtot.values():
                if d.val == 0 or e.seen.get(d.sem.num, 0) >= d.val: continue
                e.e.wait_ge(d.sem, d.val); e.seen[d.sem.num] = d.val

    def carve(self, nbytes):
        o = self.off; self.off += (nbytes + 63) // 64 * 64
        return o

    def view(self, off, nbytes, dt, pattern=None, **kw):
        ap = self.big[:, off // 2:(off + nbytes) // 2]
        if dt == F32: ap = ap.bitcast(F32)
        if pattern: ap = ap.rearrange(pattern, **kw)
        return ap

    def ws_init(self, wst):
        self.wst = wst; self.ws_i = 0; self.ws_issued = 0
        self.ws_total = self.nl * NCH

    def ws_ensure(self, cmax):
        while self.ws_issued <= min(cmax, self.ws_total - 1):
            c = self.ws_issued; s = c % NSLOT
            self.dma(self.pool, self.wslot[s], self.wst[c], self.wsem[s], writes=[self.wbuf[s]])
            self.ws_issued += 1

    def wnext(self, expect=None):
        li, ti = divmod(self.ws_i, len(TILES))
        key, nk, ncol = TILES[ti]
        if expect is not None: assert key[0] == expect, (key, expect)
        c, off = TPOS[ti]; c += li * NCH
        self.ws_i += 1
        self.ws_ensure(c + PREF)
        s = c % NSLOT
        ap = self.wslot[s][:, off:off + nk * ncol].rearrange("p (k c) -> p k c", k=nk)
        return ap, self.wbuf[s]

    def tmp(self, dt=F32):
        i = self.tmp_i; self.tmp_i = (i + 1) % len(self.tmpb)
        ap = self.view(self.tmp_off + i * 2048, 2048 if dt == F32 else 1024, dt)
        return ap, self.tmpb[i]

    def psum(self, pool="g"):
        lst = self.pspools[pool]
        i = self.ps_i[pool]; self.ps_i[pool] = (i + 1) % len(lst)
        b = lst[i]
        return self.pst[b], self.psb[b]

    def build(self):
        nc = self.nc; es = self.es
        P = self
        if self.first:
            xc = nc.dram_tensor("xc", [NT, D], F32, kind="ExternalInput").ap()
        else:
            xs_in = nc.dram_tensor("xs_in", [128, NK * NT], F32, kind="ExternalInput").ap()
        if self.last:
            out = nc.dram_tensor("out", [L, D], F32, kind="ExternalOutput").ap()
        else:
            xs_out = nc.dram_tensor("xs_out", [128, NK * NT], F32, kind="ExternalOutput").ap()
        wst = nc.dram_tensor("wst", [self.nl * NCH, 128, CH], F32, kind="ExternalInput").ap()
        vecs = nc.dram_tensor("vecs", [NVEC, 128], F32, kind="ExternalInput").ap()
        d_ident = nc.dram_tensor("ident", [128, 128], F32, kind="ExternalInput").ap()
        d_cos = nc.dram_tensor("cosT", [128, L], F32, kind="ExternalInput").ap()
        d_sin = nc.dram_tensor("sinS", [128, L], F32, kind="ExternalInput").ap()
        d_masks = nc.dram_tensor("masks", [128, 256], F32, kind="ExternalInput").ap()
        d_edge = nc.dram_tensor("edge", [128, 128], F32, kind="ExternalInput").ap()
        d_bsp = nc.dram_tensor("bsp", [128, self.nl * 512], F32, kind="ExternalInput").ap()
        d_sink = nc.dram_tensor("sink", [1, self.nl * 8], F32, kind="ExternalInput").ap()
        xspill = nc.dram_tensor("xspill", [128, NK * NT], F32, kind="Internal").ap()

        self.pe = Eng(nc.tensor, self.sem("s_pe"), "pe", same=False)
        self.act = Eng(nc.scalar, self.sem("s_act"), "act")
        self.dve = Eng(nc.vector, self.sem("s_dve"), "dve")
        self.pool = Eng(nc.gpsimd, self.sem("s_pool"), "pool")
        self.sp = Eng(nc.sync, None, "sp")
        pe, act, dve, pool, sp = self.pe, self.act, self.dve, self.pool, self.sp

        TOTAL = 203 * 1024
        self.big = es.enter_context(nc.sbuf_tensor("big", [128, TOTAL // 2], BF16)).ap()
        self.off = 0
        o_X = self.carve(NK * NT * 4)
        o_H = self.carve(NK * NT * 2)
        o_R = self.carve(NK * NT * 2)
        o_W = self.carve(NSLOT * CH * 2)
        self.tmp_off = self.carve(8 * 2048)
        o_acc = self.carve(NT * 4)
        o_vecT = self.carve(NVEC * 4)
        o_ident = self.carve(128 * 4)
        o_identb = self.carve(128 * 2)
        o_ones = self.carve(128 * 4)
        o_masks = self.carve(256 * 2)
        o_edge = self.carve(128 * 4)
        o_bsp = self.carve(512 * 4)
        o_small = self.carve(2048)
        o_mod = self.carve(2 * 48 * 4)
        o_der = self.carve(2 * 6 * 8 * 4)
        o_sc = self.carve(16 * 2)
        o_es = self.carve(1024 * 2 + 64)
        o_E = self.carve(128 * 2)
        o_sk = self.carve(64)
        o_rs = self.carve(2 * 2048)
        assert self.off <= TOTAL, self.off

        xT = self.view(o_X, NK * NT * 4, F32, "p (k t) -> p k t", k=NK)
        B_x = Buf("xT")
        yattn = self.view(o_X, 4 * NT * 2, BF16, "p (k t) -> p k t", k=4); B_yattn = Buf("yattn")
        ysg = self.view(o_X + 4 * NT * 2, 4 * NT * 2, BF16, "p (k t) -> p k t", k=4); B_ysg = Buf("ysg")
        ypool = self.view(o_X + 8 * NT * 2, 4 * NT * 2, BF16, "p (k t) -> p k t", k=4); B_ypool = Buf("ypool")
        o_vn = o_X + 12 * NT * 2
        vn = self.view(o_vn, 18 * 512 * 2, BF16, "p (n c) -> p n c", n=18); B_vn = Buf("vn")
        cosT = self.view(o_vn, L * 4, F32); sinS = self.view(o_vn + L * 4, L * 4, F32); B_rope = Buf("rope")
        hT = self.view(o_H, NK * NT * 2, BF16, "p (k t) -> p k t", k=NK); B_h = Buf("hT")
        QT = self.view(o_R, 4 * NT * 2, BF16, "p (k t) -> p k t", k=4); B_q = Buf("QT")
        KT2 = self.view(o_R + 4 * NT * 2, 2 * NT * 2, BF16, "p (k t) -> p k t", k=2); B_k = Buf("KT2")
        VA = self.view(o_R + 6 * NT * 2, 18 * 256 * 2, BF16, "p (n h c) -> p n h c", n=18, h=2); B_v = Buf("VA")
        apad = [self.view(o_R + i * APLEN * 4, APLEN * 4, F32) for i in range(3)]
        B_ap = [Buf("apad%d" % i) for i in range(3)]
        plT = self.view(o_R + 3 * APLEN * 4, NT * 2, BF16); B_pl = Buf("plT")
        yT = self.view(o_R, NK * NT * 2, BF16, "p (k t) -> p k t", k=NK); B_y = Buf("yT")
        actT = self.view(o_R, NK * NT * 2, BF16, "p (k t) -> p k t", k=NK); B_a = Buf("actT")
        self.wslot = [self.view(o_W + s * CH * 2, CH * 2, BF16) for s in range(NSLOT)]
        self.wbuf = [Buf("w%d" % s) for s in range(NSLOT)]
        self.wsem = [self.dsem("s_w%d" % s) for s in range(NSLOT)]
        self.tmpb = [Buf("tmp%d" % i) for i in range(8)]; self.tmp_i = 0
        acc = self.view(o_acc, NT * 4, F32); B_acc = Buf("acc")
        vecT = self.view(o_vecT, NVEC * 4, F32); B_vec = Buf("vecT")
        ident = self.view(o_ident, 512, F32); B_c = Buf("consts")
        ones = self.view(o_ones, 512, F32)
        masks = self.view(o_masks, 512, BF16, "p (a q) -> p a q", a=2)
        edge = self.view(o_edge, 512, F32, "p (g s d e) -> p g s d e", g=4, s=2, d=2)
        bsp = self.view(o_bsp, 2048, F32, "p (g q) -> p g q", g=4); B_bsp = Buf("bsp")
        small = self.view(o_small, 2048, F32); B_small = [Buf("small%d" % i) for i in range(8)]
        self.small_i = 0
        modv = self.view(o_mod, 2 * 48 * 4, F32, "p (t j) -> p t j", t=2); B_mod = Buf("mod")
        der = self.view(o_der, 2 * 6 * 8 * 4, F32, "p (t w k) -> p t w k", t=2, w=6); B_der = Buf("der")
        scT = self.view(o_sc, 32, BF16, "p (k t) -> p k t", k=8); B_sc = Buf("scT")
        esrow = self.view(o_es, 2048, BF16, "p (h c) -> p h c", h=2); B_es = Buf("esrow")
        Esel = self.view(o_E, 256, BF16)
        sk = self.view(o_sk, 64, F32); B_sk = Buf("sk")
        eps_t = small[:, 500:501]
        rsv = [self.view(o_rs + i * 2048, 2048, F32) for i in range(2)]; B_rs = [Buf('rs0'), Buf('rs1')]; self.rs_i = 0

        self.pst = [es.enter_context(nc.psum_tensor("ps%d" % i, [128, 512], F32)).ap() for i in range(8)]
        self.psb = [Buf("ps%d" % i) for i in range(8)]
        self.pspools = {"g": [0, 1, 2, 3, 4, 5, 6, 7], "s": [0, 1, 2, 3, 6, 7], "a": [4, 5], "o": [6, 7]}
        self.ps_i = {k: 0 for k in self.pspools}

        ds_c = self.dsem("s_const"); ds_x = [self.dsem("s_x%d" % i) for i in range(2)]
        ds_o = self.dsem("s_out"); ds_sp = self.dsem("s_spill")

        self.ws_init(wst)
        self.ws_ensure(PREF)

        self.dma(sp, ident, d_ident, ds_c, writes=[B_c])
        self.dma(sp, edge.rearrange("p g s d e -> p (g s d e)"), d_edge, ds_c, writes=[B_c])
        self.dma(pool, masks.rearrange("p a q -> p (a q)"), d_masks, ds_c, writes=[B_c])
        self.op(dve, lambda e: e.memset(ones, 1.0), writes=[B_c])
        self.op(dve, lambda e: e.memset(small, 0.0), writes=B_small)
        self.op(dve, lambda e: e.memset(eps_t, EPS), writes=B_small)
        self.op(dve, lambda e: e.memset(Esel[0:1, 0:64], 0.0), writes=[B_c])
        self.op(dve, lambda e: e.memset(Esel[0:1, 64:128], 1.0), writes=[B_c])
        for r in range(NVEC // 128):
            t_ap, t_b = self.tmp()
            self.dma(sp, t_ap[:, 0:128], vecs[r * 128:(r + 1) * 128, :], ds_c, writes=[t_b])
            ps, pb = self.psum()
            self.op(pe, lambda e: e.transpose(ps[:, 0:128], t_ap[:, 0:128], ident), reads=[t_b, B_c], writes=[pb])
            self.op(act, lambda e: e.copy(out=vecT[:, r * 128:(r + 1) * 128], in_=ps[:, 0:128]), reads=[pb], writes=[B_vec])
        self.op(act, lambda e: e.activation(out=scT[:, :, 0], in_=vecT[:, 288:296], func=AF.Silu), reads=[B_vec], writes=[B_sc])
        self.op(act, lambda e: e.activation(out=scT[:, :, 1], in_=vecT[:, 296:304], func=AF.Silu), reads=[B_vec], writes=[B_sc])

        if self.first:
            for n in range(NT // 128):
                t_ap = self.view(self.tmp_off + (n % 2) * 4096, 4096, F32); t_b = [self.tmpb[(n % 2) * 2], self.tmpb[(n % 2) * 2 + 1]]
                self.dma(sp, t_ap, xc[n * 128:(n + 1) * 128, :], ds_x[n % 2], writes=t_b)
                for hb in range(2):
                    ps, pb = self.psum()
                    for q in range(4):
                        kk = hb * 4 + q
                        self.op(pe, lambda e: e.transpose(ps[:, q * 128:(q + 1) * 128], t_ap[:, kk * 128:(kk + 1) * 128], ident),
                                reads=t_b + [B_c], writes=[pb], signal=(q == 3))
                    self.op(act if hb == 0 else dve,
                            (lambda e: e.copy(out=xT[:, hb * 4:hb * 4 + 4, n * 128:(n + 1) * 128], in_=ps.rearrange("p (a b) -> p a b", a=4))) if hb == 0 else
                            (lambda e: e.tensor_copy(out=xT[:, hb * 4:hb * 4 + 4, n * 128:(n + 1) * 128], in_=ps.rearrange("p (a b) -> p a b", a=4))),
                            reads=[pb], writes=[B_x])
            self.tmp_i = 4
        else:
            self.dma(sp, xT.rearrange("p k t -> p (k t)"), xs_in, ds_x[0], writes=[B_x])

        def smallv(n=1):
            i = self.small_i; self.small_i = (i + 1) % 8
            return small[:, i * 8:i * 8 + n], B_small[i]

        def stats_rstd(src, src_b, t0, n):
            ps, pb = self.psum()
            for kk in range(NK):
                sq, sqb = self.tmp()
                self.op(act, lambda e: e.activation(out=sq[:, 0:n], in_=src[:, kk, t0:t0 + n], func=AF.Square), reads=[src_b], writes=[sqb])
                self.op(pe, lambda e: e.matmul(ps[:, 0:n], lhsT=ones, rhs=sq[:, 0:n], start=(kk == 0), stop=(kk == NK - 1)),
                        reads=[sqb, B_c], writes=[pb], signal=(kk == NK - 1))
            r1, r1b = self.tmp()
            self.op(act, lambda e: e.activation(out=r1[:, 0:n], in_=ps[:, 0:n], func=AF.Sqrt, bias=eps_t, scale=1.0 / D), reads=[pb] + B_small, writes=[r1b])
            rs, rsb = rsv[self.rs_i], B_rs[self.rs_i]; self.rs_i ^= 1
            self.op(dve, lambda e: e.reciprocal(out=rs[:, 0:n], in_=r1[:, 0:n]), reads=[r1b], writes=[rsb])
            return rs, rsb

        def norm_mod(wG, wS):
            for (t0, n) in TT:
                ti = 0 if t0 < L else 1
                rs, rsb = stats_rstd(xT, B_x, t0, n)
                for kk in range(NK):
                    tm, tmb = self.tmp()
                    self.op(dve, lambda e: e.tensor_tensor(out=tm[:, 0:n], in0=xT[:, kk, t0:t0 + n], in1=rs[:, 0:n], op=ALU.mult),
                            reads=[B_x, rsb], writes=[tmb])
                    self.op(act, lambda e: e.activation(out=hT[:, kk, t0:t0 + n], in_=tm[:, 0:n], func=AF.Identity,
                                                        bias=der[:, ti, wS, kk:kk + 1], scale=der[:, ti, wG, kk:kk + 1]),
                            reads=[tmb, B_der], writes=[B_h])

        def proj_fm(w, wb, src, src_b, nk, t0, n, pool="g"):
            ps, pb = self.psum(pool)
            for kk in range(nk):
                self.op(pe, lambda e: e.matmul(ps[:, 0:n], lhsT=w[:, kk, :], rhs=src[:, kk, t0:t0 + n], start=(kk == 0), stop=(kk == nk - 1)),
                        reads=[wb, src_b], writes=[pb], signal=(kk == nk - 1))
            return ps, pb

        def rope_evac(ps, pb, dst, dst_b, t0, n):
            t1, t1b = self.tmp(); t2, t2b = self.tmp()
            self.op(dve, lambda e: e.tensor_tensor(out=t1[:, 0:n], in0=ps[:, 0:n], in1=cosT[:, t0:t0 + n], op=ALU.mult), reads=[pb, B_rope], writes=[t1b])
            for (o, i) in ((0, 32), (32, 0), (64, 96), (96, 64)):
                self.op(dve, lambda e: e.tensor_tensor(out=t2[o:o + 32, 0:n], in0=ps[i:i + 32, 0:n], in1=sinS[i:i + 32, t0:t0 + n], op=ALU.mult),
                        reads=[pb, B_rope], writes=[t2b])
            self.op(pool, lambda e: e.tensor_tensor(out=dst, in0=t1[:, 0:n], in1=t2[:, 0:n], op=ALU.add), reads=[t1b, t2b], writes=[dst_b])

        def stop(tag):
            if self.stop_at == tag:
                raise StopBuild()
        self.stop = stop
        try:
            stop('load')
            for li in range(self.nl):
                vb = 72 * li
                self.dma(sp, bsp.rearrange("p g q -> p (g q)"), d_bsp[:, li * 512:(li + 1) * 512], ds_c, writes=[B_bsp])
                self.dma(sp, sk[0:1, 0:8], d_sink[0:1, li * 8:(li + 1) * 8], ds_c, writes=[B_sk])
                psm, psmb = self.psum()
                psv = psm[:, 0:96].rearrange("p (t j) -> p t j", t=2)
                for j in range(48):
                    w, wb = self.wnext("mod")
                    for kk in range(NK):
                        self.op(pe, lambda e: e.matmul(psv[:, :, j], lhsT=w[:, kk, :], rhs=scT[:, kk, :], start=(kk == 0), stop=(kk == NK - 1)),
                                reads=[wb, B_sc], writes=[psmb], signal=(kk == NK - 1))
                for ti in range(2):
                    self.op(dve, lambda e: e.tensor_tensor(out=modv[:, ti, :], in0=psv[:, ti, :], in1=vecT[:, vb:vb + 48], op=ALU.add),
                            reads=[psmb, B_vec], writes=[B_mod])
                for ti in range(2):
                    m6 = modv[:, ti, :].rearrange("p (w k) -> p w k", w=6)
                    self.op(dve, lambda e: e.scalar_tensor_tensor(out=der[:, ti, 0, :], in0=m6[:, 1, :], scalar=1.0, in1=vecT[:, vb + 48:vb + 56], op0=ALU.add, op1=ALU.mult),
                            reads=[B_mod, B_vec], writes=[B_der])
                    self.op(dve, lambda e: e.scalar_tensor_tensor(out=der[:, ti, 3, :], in0=m6[:, 4, :], scalar=1.0, in1=vecT[:, vb + 56:vb + 64], op0=ALU.add, op1=ALU.mult),
                            reads=[B_mod, B_vec], writes=[B_der])
                    for (wd, ws_) in ((1, 0), (2, 2), (4, 3), (5, 5)):
                        self.op(dve, lambda e: e.tensor_copy(out=der[:, ti, wd, :], in_=m6[:, ws_, :]), reads=[B_mod], writes=[B_der])
                self.op(act, lambda e: e.activation(out=esrow[0:1].rearrange("p h (m q) -> p (h m) q", m=4),
                                                    in_=sk[0:1, 0:8].unsqueeze(2).broadcast_to([1, 8, 128]), func=AF.Exp),
                        reads=[B_sk], writes=[B_es])

                stop('mod')
                norm_mod(0, 1)
                stop('norm1')
                self.barrier()
                self.dma(sp, xspill, xT.rearrange("p k t -> p (k t)"), ds_sp, reads=[B_x])
                sp.e.wait_ge(ds_sp.sem, ds_sp.val); sp.seen[ds_sp.sem.num] = ds_sp.val
                self.barrier()
                self.dma(sp, cosT, d_cos, ds_c, writes=[B_rope])
                self.dma(sp, sinS, d_sin, ds_c, writes=[B_rope])
                self.op(pool, lambda e: e.memset(VA[:, :, :, 64:128], 1.0), writes=[B_v])

                for hk in range(2):
                    w, wb = self.wnext("k")
                    for (t0, n) in TT:
                        ps, pb = proj_fm(w, wb, hT, B_h, NK, t0, n)
                        if t0 < L: rope_evac(ps, pb, KT2[:, hk, t0:t0 + n], B_k, t0, n)
                        else: self.op(act, lambda e: e.copy(out=KT2[:, hk, t0:t0 + n], in_=ps[:, 0:n]), reads=[pb], writes=[B_k])
                for j in range(4):
                    w, wb = self.wnext("q")
                    for (t0, n) in TT:
                        ps, pb = proj_fm(w, wb, hT, B_h, NK, t0, n)
                        if t0 < L: rope_evac(ps, pb, QT[:, j, t0:t0 + n], B_q, t0, n)
                        else: self.op(act, lambda e: e.copy(out=QT[:, j, t0:t0 + n], in_=ps[:, 0:n]), reads=[pb], writes=[B_q])
                w, wb = self.wnext("v")
                for n0 in range(0, 18, 4):
                    nb = min(4, 18 - n0)
                    ps, pb = self.psum()
                    for i in range(nb):
                        n = n0 + i
                        for kk in range(NK):
                            self.op(pe, lambda e: e.matmul(ps[:, i * 128:(i + 1) * 128], lhsT=hT[:, kk, n * 128:(n + 1) * 128], rhs=w[:, kk, :],
                                                           start=(kk == 0), stop=(kk == NK - 1)),
                                    reads=[wb, B_h], writes=[pb], signal=(kk == NK - 1 and i == nb - 1))
                    self.op(act, lambda e: e.copy(out=VA[:, n0:n0 + nb, :, 0:64], in_=ps[:, 0:nb * 128].rearrange("p (n h c) -> p n h c", n=nb, h=2)),
                            reads=[pb], writes=[B_v])

                stop('qkv')
                import os
                KATT = int(os.environ.get("KATT", "9"))

                def attn_block(qt0, keyblocks, hk, dst_tok):
                    accp, accb = self.psum("a")
                    nkb = len(keyblocks)
                    for bi, (kt0, n_kb, mk) in enumerate(keyblocks):
                        sA, sAb = self.psum("s"); sB, sBb = self.psum("s")
                        sbank = ((sA, sAb), (sB, sBb))
                        for m in range(4):
                            h = 4 * hk + m; j2 = h // 2; e2 = h % 2
                            sp_, spb_ = sbank[e2]
                            self.op(pe, lambda e: e.matmul(sp_[:, (m // 2) * 128:(m // 2 + 1) * 128], lhsT=KT2[64 * e2:64 * e2 + 64, hk, kt0:kt0 + 128],
                                                           rhs=QT[64 * e2:64 * e2 + 64, j2, qt0:qt0 + 128], start=True, stop=True),
                                    reads=[B_k, B_q], writes=[spb_], signal=(m >= 2))
                        if KATT < 1: continue
                        pt, ptb = self.tmp(BF16)
                        pt4 = pt.rearrange("p (a b q) -> p a b q", a=2, b=2)
                        for e2 in range(2):
                            sp_, spb_ = sbank[e2]
                            self.op(act, lambda e: e.activation(out=pt4[:, :, e2, :], in_=sp_[:, 0:256].rearrange("p (a q) -> p a q", a=2), func=AF.Exp, scale=0.125),
                                    reads=[spb_], writes=[ptb])
                        if mk is not None and KATT >= 2:
                            ptv = pt.rearrange("p (m q) -> p m q", m=4)
                            self.op(dve, lambda e: e.tensor_tensor(out=ptv, in0=ptv, in1=masks[:, mk, :].unsqueeze(1).broadcast_to([128, 4, 128]), op=ALU.mult),
                                    reads=[ptb, B_c], writes=[ptb])
                        if KATT >= 3:
                            lastmm = (KATT == 3 and bi == nkb - 1)
                            self.op(pe, lambda e: e.matmul(accp, lhsT=VA[:, n_kb, hk, :], rhs=pt, start=(bi == 0), stop=lastmm),
                                    reads=[B_v, ptb], writes=[accb], signal=lastmm)
                    if KATT >= 4:
                        self.op(pe, lambda e: e.matmul(accp, lhsT=Esel[0:1, :], rhs=esrow[0:1, hk, :], start=False, stop=True),
                                reads=[B_c, B_es], writes=[accb], signal=True)
                    if KATT >= 5:
                        rd, rdb = self.tmp()
                        self.op(dve, lambda e: e.reciprocal(out=rd[0:64, :], in_=accp[64:128, :]), reads=[accb], writes=[rdb])
                    if KATT >= 6:
                        av = accp[0:64, :].rearrange("p (a b q) -> p a b q", a=2, b=2)
                        rv = rd[0:64, :].rearrange("p (a b q) -> p a b q", a=2, b=2)
                        for par in range(2):
                            self.op(dve, lambda e: e.tensor_tensor(out=yattn[64 * par:64 * par + 64, 2 * hk:2 * hk + 2, dst_tok:dst_tok + 128],
                                                                   in0=av[:, :, par, :], in1=rv[:, :, par, :], op=ALU.mult),
                                    reads=[accb, rdb], writes=[B_yattn])

                NBLK = int(os.environ.get('KNBLK', '16'))
                for i in range(NBLK):
                    kbs = []
                    for j in (i - 1, i, i + 1):
                        if 0 <= j < 16:
                            kbs.append((j * 128, j, None if j == i else (0 if j < i else 1)))
                    kbs += [(L, 16, None), (L + 128, 17, None)]
                    for hk in range(2):
                        attn_block(i * 128, kbs, hk, i * 128)
                for ic in range(2):
                    for hk in range(2):
                        attn_block(L + ic * 128, [(L, 16, None), (L + 128, 17, None)], hk, L + ic * 128)
                self.barrier()

                stop('attn')
                wsv = [self.wnext("sv") for _ in range(2)]
                for n in range(18):
                    ps, pb = self.psum()
                    for h2 in range(2):
                        w, wb = wsv[h2]
                        for kk in range(NK):
                            self.op(pe, lambda e: e.matmul(ps[:, h2 * 256:(h2 + 1) * 256], lhsT=hT[:, kk, n * 128:(n + 1) * 128], rhs=w[:, kk, :],
                                                           start=(kk == 0), stop=(kk == NK - 1)),
                                    reads=[wb, B_h], writes=[pb], signal=(kk == NK - 1 and h2 == 1))
                    gv, gvb = self.tmp()
                    self.op(act, lambda e: e.activation(out=gv, in_=ps, func=AF.Gelu_apprx_tanh), reads=[pb], writes=[gvb])
                    jk, jkb = self.tmp()
                    ss, ssb = smallv()
                    self.op(act, lambda e: e.activation(out=jk, in_=gv, func=AF.Square, accum_out=ss), reads=[gvb], writes=[jkb, ssb])
                    s2, s2b = smallv()
                    self.op(act, lambda e: e.activation(out=s2, in_=ss, func=AF.Sqrt, bias=eps_t, scale=1.0 / 512), reads=[ssb] + B_small[7:8], writes=[s2b])
                    s3, s3b = smallv()
                    self.op(dve, lambda e: e.reciprocal(out=s3, in_=s2), reads=[s2b], writes=[s3b])
                    self.op(dve, lambda e: e.tensor_scalar(out=vn[:, n, :], in0=gv, scalar1=s3, scalar2=None, op0=ALU.mult), reads=[gvb, s3b], writes=[B_vn])
                for g in range(4):
                    w, wb = self.wnext("u")
                    for (t0, n) in TT:
                        ps, pb = proj_fm(w, wb, hT, B_h, NK, t0, n)
                        self.op(act, lambda e: e.activation(out=ysg[:, g, t0:t0 + n], in_=ps[:, 0:n], func=AF.Gelu_apprx_tanh), reads=[pb], writes=[B_ysg])
                wws = [self.wnext("ws") for _ in range(4)]
                for g in range(4):
                    w, wb = wws[g]
                    for (t0, n) in TT:
                        nb = n // 128; c0 = t0 // 128
                        ps, pb = self.psum()
                        for i in range(nb):
                            self.op(pe, lambda e: e.matmul(ps[:, i * 128:(i + 1) * 128], lhsT=vn[:, c0 + i, g * 128:(g + 1) * 128], rhs=w[:, 0, :], start=True, stop=True),
                                    reads=[wb, B_vn], writes=[pb], signal=(i == nb - 1))
                        tm, tmb = self.tmp()
                        self.op(dve, lambda e: e.scalar_tensor_tensor(out=tm[:, 0:n].rearrange("p (i q) -> p i q", i=nb),
                                                                      in0=ps[:, 0:n].rearrange("p (i q) -> p i q", i=nb),
                                                                      scalar=vecT[:, vb + 68 + g:vb + 69 + g],
                                                                      in1=bsp[:, g, :].unsqueeze(1).broadcast_to([128, nb, 128]), op0=ALU.mult, op1=ALU.add),
                                reads=[pb, B_vec, B_bsp], writes=[tmb])
                        self.op(pool, lambda e: e.tensor_tensor(out=ysg[:, g, t0:t0 + n], in0=ysg[:, g, t0:t0 + n], in1=tm[:, 0:n], op=ALU.mult),
                                reads=[tmb, B_ysg], writes=[B_ysg])
                self.barrier()

                stop('sg')
                for i3 in range(3):
                    self.op(pool, lambda e: e.memset(apad[i3], 0.0), writes=[B_ap[i3]])
                A = apad[0]

                def seqpos(t0):
                    return XPAD + t0 if t0 < L else CTX_OFF + (t0 - L)
                for g in range(4):
                    wd = POOLW[g]; hw = wd // 2
                    w, wb = self.wnext("pc")
                    for (t0, n) in TT:
                        ps, pb = proj_fm(w, wb, hT, B_h, NK, t0, n)
                        p0 = seqpos(t0)
                        self.op(act, lambda e: e.copy(out=A[:, p0:p0 + n], in_=ps[:, 0:n]), reads=[pb], writes=[B_ap[0]])
                    cur, curb = A, B_ap[0]; step = 1; idx = 1
                    while step < wd:
                        nxt, nxtb = apad[idx], B_ap[idx]
                        ln = APLEN - step
                        self.op(dve, lambda e: e.tensor_tensor(out=nxt[:, 0:ln], in0=cur[:, 0:ln], in1=cur[:, step:step + ln], op=ALU.add),
                                reads=[curb], writes=[nxtb])
                        cur, curb = nxt, nxtb; step *= 2; idx = 3 - idx
                    for si, (s0, Ls, tok0) in enumerate(((XPAD, L, 0), (CTX_OFF, C, L))):
                        for c0 in range(0, Ls, 1024):
                            ln = min(1024, Ls - c0)
                            self.op(dve, lambda e: e.scalar_tensor_tensor(out=plT[:, tok0 + c0:tok0 + c0 + ln], in0=cur[:, s0 + c0 - hw:s0 + c0 - hw + ln], scalar=1.0 / wd,
                                                                          in1=A[:, s0 + c0:s0 + c0 + ln], op0=ALU.mult, op1=ALU.subtract),
                                    reads=[curb, B_ap[0]], writes=[B_pl])
                        for side, (e0, ne) in enumerate(((0, hw), (Ls - hw + 1, hw - 1))):
                            if ne <= 0: continue
                            sv_, svb = smallv(8)
                            self.op(dve, lambda e: e.tensor_tensor(out=sv_[:, 0:ne], in0=cur[:, s0 + e0 - hw:s0 + e0 - hw + ne], in1=edge[:, g, si, side, 0:ne], op=ALU.mult),
                                    reads=[curb, B_c], writes=[svb])
                            self.op(dve, lambda e: e.tensor_tensor(out=plT[:, tok0 + e0:tok0 + e0 + ne], in0=sv_[:, 0:ne], in1=A[:, s0 + e0:s0 + e0 + ne], op=ALU.subtract),
                                    reads=[svb, B_ap[0]], writes=[B_pl])
                    w2, w2b = self.wnext("wp")
                    for (t0, n) in TT:
                        ps, pb = self.psum()
                        self.op(pe, lambda e: e.matmul(ps[:, 0:n], lhsT=w2[:, 0, :], rhs=plT[:, t0:t0 + n], start=True, stop=True), reads=[w2b, B_pl], writes=[pb])
                        self.op(act, lambda e: e.activation(out=ypool[:, g, t0:t0 + n], in_=ps[:, 0:n], func=AF.Identity, scale=vecT[:, vb + 64 + g:vb + 65 + g]),
                                reads=[pb, B_vec], writes=[B_ypool])
                self.barrier()

                stop('pool')
                ysrc = [(ypool, B_ypool), (yattn, B_yattn), (ysg, B_ysg)]
                for j in range(8):
                    for b in range(3):
                        wg, wgb = self.wnext("gate")
                        wr, wrb = self.wnext("br")
                        for (t0, n) in TT:
                            psg, pgb = proj_fm(wg, wgb, hT, B_h, NK, t0, n)
                            sg, sgb = self.tmp()
                            self.op(act, lambda e: e.activation(out=sg[:, 0:n], in_=psg[:, 0:n], func=AF.Sigmoid), reads=[pgb], writes=[sgb])
                            psb_, pbb = proj_fm(wr, wrb, ysrc[b][0], ysrc[b][1], 4, t0, n)
                            if b == 0:
                                self.op(dve, lambda e: e.tensor_tensor(out=acc[:, t0:t0 + n], in0=psb_[:, 0:n], in1=sg[:, 0:n], op=ALU.mult), reads=[pbb, sgb], writes=[B_acc])
                            else:
                                tm, tmb = self.tmp()
                                self.op(dve, lambda e: e.tensor_tensor(out=tm[:, 0:n], in0=psb_[:, 0:n], in1=sg[:, 0:n], op=ALU.mult), reads=[pbb, sgb], writes=[tmb])
                                if b == 1:
                                    self.op(pool, lambda e: e.tensor_tensor(out=acc[:, t0:t0 + n], in0=acc[:, t0:t0 + n], in1=tm[:, 0:n], op=ALU.add), reads=[tmb, B_acc], writes=[B_acc])
                                else:
                                    self.op(pool, lambda e: e.tensor_tensor(out=yT[:, j, t0:t0 + n], in0=acc[:, t0:t0 + n], in1=tm[:, 0:n], op=ALU.add), reads=[tmb, B_acc], writes=[B_y])
                self.barrier()
                self.dma(sp, xT.rearrange("p k t -> p (k t)"), xspill, ds_sp, writes=[B_x])
                sp.e.wait_ge(ds_sp.sem, ds_sp.val); sp.seen[ds_sp.sem.num] = ds_sp.val
                self.barrier()

                stop('merge')
                for j in range(8):
                    w, wb = self.wnext("out")
                    for (t0, n) in TT:
                        ti = 0 if t0 < L else 1
                        ps, pb = proj_fm(w, wb, yT, B_y, NK, t0, n)
                        self.op(dve, lambda e: e.scalar_tensor_tensor(out=xT[:, j, t0:t0 + n], in0=ps[:, 0:n], scalar=der[:, ti, 2, j:j + 1], in1=xT[:, j, t0:t0 + n],
                                                                      op0=ALU.mult, op1=ALU.add), reads=[pb, B_der, B_x], writes=[B_x])
                stop('outproj')
                norm_mod(3, 4)
                self.barrier()
                for gi, grp in enumerate(FFG):
                    for fi, f in enumerate(grp):
                        wg, wgb = self.wnext("fg")
                        wu, wub = self.wnext("fu")
                        for (t0, n) in TT:
                            psg, pgb = proj_fm(wg, wgb, hT, B_h, NK, t0, n)
                            sg, sgb = self.tmp()
                            self.op(act, lambda e: e.activation(out=sg[:, 0:n], in_=psg[:, 0:n], func=AF.Silu), reads=[pgb], writes=[sgb])
                            psu, pub = proj_fm(wu, wub, hT, B_h, NK, t0, n)
                            self.op(dve, lambda e: e.tensor_tensor(out=actT[:, fi, t0:t0 + n], in0=psu[:, 0:n], in1=sg[:, 0:n], op=ALU.mult), reads=[pub, sgb], writes=[B_a])
                    for j in range(8):
                        w, wb = self.wnext("fo")
                        for (t0, n) in TT:
                            ti = 0 if t0 < L else 1
                            ps, pb = proj_fm(w, wb, actT, B_a, len(grp), t0, n)
                            self.op(dve, lambda e: e.scalar_tensor_tensor(out=xT[:, j, t0:t0 + n], in0=ps[:, 0:n], scalar=der[:, ti, 5, j:j + 1], in1=xT[:, j, t0:t0 + n],
                                                                          op0=ALU.mult, op1=ALU.add), reads=[pb, B_der, B_x], writes=[B_x])
                self.barrier()

        except StopBuild:
            pass

        if self.last and self.stop_at is None:
            fgc = 304
            for (t0, n) in TT[:4]:
                rs, rsb = stats_rstd(xT, B_x, t0, n)
                for kk in range(NK):
                    self.op(dve, lambda e: e.scalar_tensor_tensor(out=xT[:, kk, t0:t0 + n], in0=xT[:, kk, t0:t0 + n], scalar=vecT[:, fgc + kk:fgc + kk + 1], in1=rs[:, 0:n],
                                                                  op0=ALU.mult, op1=ALU.mult), reads=[B_x, B_vec, rsb], writes=[B_x])
            self.barrier()
            ob = [Buf("o0"), Buf("o1")]
            for n in range(L // 128):
                o_ap = self.view(o_H + (n % 2) * 4096, 4096, F32)
                for hb in range(2):
                    ps, pb = self.psum()
                    for q in range(4):
                        kk = hb * 4 + q
                        self.op(pe, lambda e: e.transpose(ps[:, q * 128:(q + 1) * 128], xT[:, kk, n * 128:(n + 1) * 128], ident),
                                reads=[B_x, B_c], writes=[pb], signal=(q == 3))
                    if hb == 0:
                        self.op(act, lambda e: e.copy(out=o_ap[:, 0:512], in_=ps), reads=[pb], writes=[ob[n % 2]])
                    else:
                        self.op(dve, lambda e: e.tensor_copy(out=o_ap[:, 512:1024], in_=ps), reads=[pb], writes=[ob[n % 2]])
                self.dma(sp, out[n * 128:(n + 1) * 128, :], o_ap, ds_o, reads=[ob[n % 2]])
        else:
            self.barrier()
            if self.dump == 'X':
                self.dma(sp, xs_out, xT.rearrange("p k t -> p (k t)"), ds_o, reads=[B_x])
            else:
                o_d = o_H if self.dump == 'H' else o_R
                self.dma(sp, xs_out[:, 0:NK * NT // 2], self.view(o_d, NK * NT * 2, F32), ds_o, reads=[B_h, B_y])
        sp.e.wait_ge(ds_o.sem, ds_o.val)
        self.barrier()
        es.close()
        return nc


_CACHE = {}


def _prog(first, last, nl):
    key = (first, last, nl)
    if key not in _CACHE:
        _CACHE[key] = Prog(first, last, nl).build()
    return _CACHE[key]


def _vecs(W, b):
    v = np.zeros((NVEC, 128), np.float32)
    for l in range(DEPTH):
        base = 72 * l
        v[base:base + 48] = W["b_mod"][l].reshape(48, 128)
        v[base + 48:base + 56] = W["norm1_gain"][l].reshape(8, 128)
        v[base + 56:base + 64] = W["norm2_gain"][l].reshape(8, 128)
        v[base + 64:base + 68] = W["pool_scale"][l].reshape(4, 128)
        v[base + 68:base + 72] = W["sg_v_gain"][l].reshape(4, 128)
    v[288:296] = W["c"][b].reshape(8, 128)
    v[296:304] = W["c_ctx"].reshape(8, 128)
    v[304:312] = W["final_gain"].reshape(8, 128)
    return v


LAUNCH_LAYERS = [[0, 1, 2, 3]]


def kernel(**inputs):
    W = {k: np.asarray(v) for k, v in inputs.items()}
    consts = const_tables()
    ncores = 8
    wst_layers = [pack_layer(l, W) for l in range(DEPTH)]
    vecs = [_vecs(W, b) for b in range(ncores)]
    state = None
    result = None
    for gi, layers in enumerate(LAUNCH_LAYERS):
        first = gi == 0; last = gi == len(LAUNCH_LAYERS) - 1
        nc = _prog(first, last, len(layers))
        wst = np.concatenate([wst_layers[l] for l in layers], axis=0)
        bsp = np.concatenate([np.broadcast_to(W["b_spatial"][l].reshape(1, 512), (128, 512)) for l in layers], axis=1).astype(np.float32)
        sink = np.concatenate([W["attn_sink"][l].reshape(1, 8) for l in layers], axis=1).astype(np.float32)
        in_maps = []
        for b in range(ncores):
            v = vecs[b].copy()
            for i, l in enumerate(layers):
                v[72 * i:72 * i + 72] = vecs[b][72 * l:72 * l + 72]
            m = dict(wst=wst, vecs=v, ident=consts["ident"], cosT=consts["cosT"], sinS=consts["sinS"], masks=consts["masks"],
                     edge=consts["edge"], bsp=np.ascontiguousarray(bsp), sink=sink)
            if first:
                m["xc"] = np.ascontiguousarray(np.concatenate([W["x"][b], W["ctx"][b]], axis=0))
            else:
                m["xs_in"] = state[b]
            in_maps.append(m)
        res = run_bass_kernel_spmd(nc, in_maps, core_ids=list(range(ncores)))
        if last:
            result = np.stack([res.results[b]["out"] for b in range(ncores)], axis=0)
        else:
            state = [res.results[b]["xs_out"] for b in range(ncores)]
    return result.astype(np.float32)
```

```python
import numpy as np
from contextlib import ExitStack
import concourse.bass as bass
import concourse.mybir as mybir
from concourse.bass_utils import run_bass_kernel_spmd

F32 = mybir.dt.float32
BF16 = mybir.dt.bfloat16
AF = mybir.ActivationFunctionType
ALU = mybir.AluOpType

D = 1024; L = 2048; C = 256; NT = L + C; DEPTH = 4; NK = 8
D_FF = 2816
OFF_Q = 512; OFF_K = 1024; OFF_V = 1152; OFF_U = 1280; OFF_SV = 1792; OFF_GATE = 2304
EPS = 1e-6
CH = 2048
NSLOT = 4; PREF = 2
TT = [(0, 512), (512, 512), (1024, 512), (1536, 512), (2048, 256)]
FFG = [list(range(0, 8)), list(range(8, 15)), list(range(15, 22))]
PERM = np.array(list(range(0, 16)) + list(range(32, 48)) + list(range(16, 32)) + list(range(48, 64)))
POOLW = (2, 4, 8, 16)
XPAD = 8
APLEN = XPAD + L + XPAD + XPAD + C + XPAD
CTX_OFF = XPAD + L + XPAD + XPAD
NVEC = 384


def plan_tiles(l, nl):
    t = []
    if l == 0:
        for j in range(48): t.append((("mod", j, 0), 8, 128))
    t.append((("v",), 8, 128))
    for hk in range(2): t.append((("k", hk), 8, 128))
    for j in range(4): t.append((("q", j), 8, 128))
    for h in range(2): t.append((("sv", h), 8, 256))
    for g in range(4): t.append((("u", g), 8, 128))
    for g in range(4): t.append((("ws", g), 1, 128))
    for g in range(4):
        t.append((("pc", g), 8, 128)); t.append((("wp", g), 1, 128))
    for j in range(8):
        for b in range(3):
            t.append((("gate", b, j), 8, 128)); t.append((("br", b, j), 4, 128))
    for j in range(8): t.append((("out", j), 8, 128))
    nm = 0
    for gi, grp in enumerate(FFG):
        for f in grp:
            t.append((("fg", f), 8, 128)); t.append((("fu", f), 8, 128))
            if l < nl - 1:
                for _ in range(MOD_PER_F):
                    if nm < 48:
                        t.append((("mod", nm, l + 1), 8, 128)); nm += 1
        for j in range(8): t.append((("fo", gi, j), len(grp), 128))
    assert l == nl - 1 or nm == 48
    return t


MOD_PER_F = 3


def pack_plan(tiles):
    pos = []; c = 0; off = 0
    for key, nk, ncol in tiles:
        sz = nk * ncol
        if off + sz > CH:
            c += 1; off = 0
        pos.append((c, off)); off += sz
    return pos, c + 1


def build_plan(nl):
    layers = []; base = 0
    for l in range(nl):
        tiles = plan_tiles(l, nl); pos, nch = pack_plan(tiles)
        layers.append(dict(tiles=tiles, pos=pos, nch=nch, base=base)); base += nch
    return dict(layers=layers, total=base)


def tile_data(key, l, W):
    k = key[0]
    if k == "mod": j = key[1]; return W["w_mod"][key[2]][:, j * 128:(j + 1) * 128]
    if k == "k":
        cols = OFF_K + key[1] * 64 + PERM; cols = np.concatenate([cols, cols]); return W["w_in"][l][:, cols]
    if k == "q":
        j = key[1]
        cols = np.concatenate([OFF_Q + (2 * j) * 64 + PERM, OFF_Q + (2 * j + 1) * 64 + PERM]); return W["w_in"][l][:, cols]
    if k == "v": return W["w_in"][l][:, OFF_V:OFF_V + 128]
    if k == "sv": h = key[1]; return W["w_in"][l][:, OFF_SV + h * 256:OFF_SV + (h + 1) * 256]
    if k == "u": g = key[1]; return W["w_in"][l][:, OFF_U + g * 128:OFF_U + (g + 1) * 128]
    if k == "ws": return W["w_spatial"][l, key[1]].T
    if k == "pc": g = key[1]; return W["w_in"][l][:, g * 128:(g + 1) * 128]
    if k == "wp": return W["w_pool"][l, key[1]]
    if k == "gate":
        b, j = key[1], key[2]; o = OFF_GATE + b * 1024 + j * 128; return W["w_in"][l][:, o:o + 128]
    if k == "br":
        b, j = key[1], key[2]; return W[("w_br_pool", "w_br_attn", "w_br_sg")[b]][l][:, j * 128:(j + 1) * 128]
    if k == "out": j = key[1]; return W["w_out"][l][:, j * 128:(j + 1) * 128]
    if k == "fg": f = key[1]; return W["w_ffn_in"][l][:, f * 128:(f + 1) * 128]
    if k == "fu": f = key[1]; return W["w_ffn_in"][l][:, D_FF + f * 128:D_FF + (f + 1) * 128]
    if k == "fo":
        gi, j = key[1], key[2]; grp = FFG[gi]
        return W["w_ffn_out"][l][grp[0] * 128:(grp[-1] + 1) * 128, j * 128:(j + 1) * 128]
    raise KeyError(key)


def pack_layer(l, W, plan):
    pl = plan["layers"][l]
    out = np.zeros((pl["nch"], 128, CH), np.float32)
    for (key, nk, ncol), (c, off) in zip(pl["tiles"], pl["pos"]):
        a = np.asarray(tile_data(key, l, W), dtype=np.float32)
        a = a.reshape(nk, 128, ncol).transpose(1, 0, 2).reshape(128, nk * ncol)
        out[c, :, off:off + nk * ncol] = a
    return out


def const_tables():
    inv_freq = (10000.0 ** (-np.arange(16, dtype=np.float32) / 16)).astype(np.float32)
    t = np.arange(L)
    row = (t // 64).astype(np.float32); col = (t % 64).astype(np.float32)
    ang = np.zeros((32, L), np.float32)
    ang[0:16] = inv_freq[:, None] * row[None, :]
    ang[16:32] = inv_freq[:, None] * col[None, :]
    cos32 = np.cos(ang).astype(np.float32); sin32 = np.sin(ang).astype(np.float32)
    cosT = np.tile(cos32, (4, 1))
    sinS = np.concatenate([sin32, -sin32, sin32, -sin32], axis=0)
    pidx = np.arange(128); partner = (pidx // 64) * 64 + (pidx % 64 + 32) % 64
    permM = np.zeros((128, 128), np.float32); permM[partner, pidx] = 1.0
    ki = np.arange(128)[:, None]; qi = np.arange(128)[None, :]
    masks = np.stack([(qi <= ki), (ki <= qi)], axis=1).astype(np.float32)
    edge = np.ones((4, 2, 2, 8), np.float32)
    for g, w in enumerate(POOLW):
        h = w // 2
        for e in range(h):
            edge[g, :, 0, e] = 1.0 / (e + h)
        for e in range(h - 1):
            edge[g, :, 1, e] = 1.0 / (h + (h - 1 - e))
    edge = np.broadcast_to(edge.reshape(1, -1), (128, 128)).copy()
    ident = np.eye(128, dtype=np.float32)
    return dict(permM=permM, cosT=cosT, sinS=sinS, masks=masks.reshape(128, 256), edge=edge, ident=ident)


class Buf:
    __slots__ = ("name", "w", "r")

    def __init__(self, name):
        self.name = name; self.w = None; self.r = {}


class Eng:
    def __init__(self, e, sem, name, same=True):
        self.e = e; self.sem = sem; self.cnt = 0; self.seen = {}; self.name = name; self.same = same


class DSem:
    def __init__(self, sem):
        self.sem = sem; self.val = 0


class StopBuild(Exception):
    pass


class Prog:
    def __init__(self, first, last, nlayers, stop_at=None, dump='X'):
        self.first = first; self.last = last; self.nl = nlayers; self.stop_at = stop_at; self.dump = dump
        self.nc = bass.Bass("TRN2", target_bir_lowering=False)
        self.es = ExitStack()
        self.dtot = {}

    def sem(self, name):
        return self.es.enter_context(self.nc.semaphore(name))

    def dsem(self, name):
        d = DSem(self.sem(name)); self.dtot[d.sem.num] = d; return d

    def waits(self, eng, reads, writes):
        need = {}

        def add(ev):
            if ev is None: return
            s, v = ev
            if s.num in self.dtot: v = max(v, self.dtot[s.num].val)
            if need.get(s.num, (None, 0))[1] < v: need[s.num] = (s, v)
        for b in reads: add(b.w)
        for b in writes:
            if b.w is not None and b.w[0] is not eng.sem: add(b.w)
            for ev in b.r.values():
                if ev[0] is not eng.sem: add(ev)
        for key, (s, v) in need.items():
            if s is eng.sem:
                if not eng.same or v > eng.cnt: continue
            if eng.seen.get(key, 0) >= v: continue
            eng.e.wait_ge(s, v); eng.seen[key] = v

    def mark(self, ev, reads, writes):
        for b in writes:
            b.w = ev; b.r = {}
        for b in reads:
            b.r[ev[0].num] = ev

    def op(self, eng, fn, reads=(), writes=(), signal=True):
        self.waits(eng, reads, writes)
        ins = fn(eng.e)
        val = eng.cnt + 1
        if signal:
            ins.then_inc(eng.sem, 1); eng.cnt = val
        self.mark((eng.sem, val), reads, writes)
        return ins

    def dma(self, q, out, in_, ds, reads=(), writes=()):
        self.waits(q, reads, writes)
        ds.val += 16
        q.e.dma_start(out=out, in_=in_).then_inc(ds.sem, 16)
        self.mark((ds.sem, ds.val), reads, writes)

    def barrier(self):
        engs = [self.pe, self.act, self.dve, self.pool, self.sp]
        for e in engs:
            for f in engs:
                if f is e or f.sem is None or f.cnt == 0: continue
                if e.seen.get(f.sem.num, 0) >= f.cnt: continue
                e.e.wait_ge(f.sem, f.cnt); e.seen[f.sem.num] = f.cnt
            for d in self.dtot.values():
                if d.val == 0 or e.seen.get(d.sem.num, 0) >= d.val: continue
                e.e.wait_ge(d.sem, d.val); e.seen[d.sem.num] = d.val

    def carve(self, nbytes):
        o = self.off; self.off += (nbytes + 63) // 64 * 64
        return o

    def view(self, off, nbytes, dt, pattern=None, **kw):
        ap = self.big[:, off // 2:(off + nbytes) // 2]
        if dt == F32: ap = ap.bitcast(F32)
        if pattern: ap = ap.rearrange(pattern, **kw)
        return ap

    def ws_init(self, wst):
        self.wst = wst; self.ws_i = 0; self.ws_issued = 0
        self.plan = build_plan(self.nl)
        self.ws_total = self.plan["total"]
        self.flat = []
        for pl in self.plan["layers"]:
            for (key, nk, ncol), (c, off) in zip(pl["tiles"], pl["pos"]):
                self.flat.append((key, nk, ncol, pl["base"] + c, off))

    def ws_ensure(self, cmax):
        while self.ws_issued <= min(cmax, self.ws_total - 1):
            c = self.ws_issued; s = c % NSLOT
            self.dma(self.pool, self.wslot[s], self.wst[c], self.wsem[s], writes=[self.wbuf[s]])
            self.ws_issued += 1

    def wnext(self, expect=None):
        key, nk, ncol, c, off = self.flat[self.ws_i]
        if expect is not None: assert key[0] == expect, (key, expect)
        self.ws_i += 1
        self.ws_ensure(c + PREF)
        s = c % NSLOT
        ap = self.wslot[s][:, off:off + nk * ncol].rearrange("p (k c) -> p k c", k=nk)
        return ap, self.wbuf[s]

    def tmp(self, dt=F32):
        i = self.tmp_i; self.tmp_i = (i + 1) % len(self.tmpb)
        ap = self.view(self.tmp_off + i * 2048, 2048 if dt == F32 else 1024, dt)
        return ap, self.tmpb[i]

    def psum(self, pool="g"):
        lst = self.pspools[pool]
        i = self.ps_i[pool]; self.ps_i[pool] = (i + 1) % len(lst)
        b = lst[i]
        return self.pst[b], self.psb[b]

    def build(self):
        nc = self.nc; es = self.es
        P = self
        if self.first:
            xc = nc.dram_tensor("xc", [NT, D], F32, kind="ExternalInput").ap()
        else:
            xs_in = nc.dram_tensor("xs_in", [128, NK * NT], F32, kind="ExternalInput").ap()
        if self.last:
            out = nc.dram_tensor("out", [L, D], F32, kind="ExternalOutput").ap()
        else:
            xs_out = nc.dram_tensor("xs_out", [128, NK * NT], F32, kind="ExternalOutput").ap()
        wst = nc.dram_tensor("wst", [build_plan(self.nl)["total"], 128, CH], F32, kind="ExternalInput").ap()
        vecs = nc.dram_tensor("vecs", [NVEC, 128], F32, kind="ExternalInput").ap()
        d_ident = nc.dram_tensor("ident", [128, 128], F32, kind="ExternalInput").ap()
        d_perm = nc.dram_tensor("permM", [128, 128], F32, kind="ExternalInput").ap()
        d_cos = nc.dram_tensor("cosT", [128, L], F32, kind="ExternalInput").ap()
        d_sin = nc.dram_tensor("sinS", [128, L], F32, kind="ExternalInput").ap()
        d_masks = nc.dram_tensor("masks", [128, 256], F32, kind="ExternalInput").ap()
        d_edge = nc.dram_tensor("edge", [128, 128], F32, kind="ExternalInput").ap()
        d_bsp = nc.dram_tensor("bsp", [128, self.nl * 512], F32, kind="ExternalInput").ap()
        d_sink = nc.dram_tensor("sink", [1, self.nl * 8], F32, kind="ExternalInput").ap()
        xspill = nc.dram_tensor("xspill", [128, NK * NT], F32, kind="Internal").ap()

        self.pe = Eng(nc.tensor, self.sem("s_pe"), "pe", same=False)
        self.act = Eng(nc.scalar, self.sem("s_act"), "act")
        self.dve = Eng(nc.vector, self.sem("s_dve"), "dve")
        self.pool = Eng(nc.gpsimd, self.sem("s_pool"), "pool")
        self.sp = Eng(nc.sync, None, "sp")
        pe, act, dve, pool, sp = self.pe, self.act, self.dve, self.pool, self.sp

        TOTAL = 203 * 1024
        self.big = es.enter_context(nc.sbuf_tensor("big", [128, TOTAL // 2], BF16)).ap()
        self.off = 0
        o_X = self.carve(NK * NT * 4)
        o_H = self.carve(NK * NT * 2)
        o_R = self.carve(NK * NT * 2)
        o_W = self.carve(NSLOT * CH * 2)
        self.tmp_off = self.carve(8 * 2048)
        o_acc = self.carve(NT * 4)
        o_vecT = self.carve(NVEC * 4)
        o_ident = self.carve(128 * 4)
        o_identb = self.carve(128 * 2)
        o_ones = self.carve(128 * 4)
        o_masks = self.carve(512 * 2)
        o_edge = self.carve(128 * 4)
        o_bsp = self.carve(512 * 4)
        o_small = self.carve(2048)
        o_mod = self.carve(2 * 48 * 4)
        o_der = self.carve(2 * 6 * 8 * 4)
        o_sc = self.carve(16 * 2)
        o_es = self.carve(1024 * 2 + 64)
        o_E = self.carve(128 * 2)
        o_sk = self.carve(64)
        o_perm = self.carve(512)
        o_rs = self.carve(2 * 2048)
        assert self.off <= TOTAL, self.off

        xT = self.view(o_X, NK * NT * 4, F32, "p (k t) -> p k t", k=NK)
        B_x = Buf("xT"); B_xk = [Buf("xT%d" % i) for i in range(NK)]
        yattn = self.view(o_X, 4 * NT * 2, BF16, "p (k t) -> p k t", k=4); B_yattn = Buf("yattn")
        ysg = self.view(o_X + 4 * NT * 2, 4 * NT * 2, BF16, "p (k t) -> p k t", k=4); B_ysg = Buf("ysg")
        ypool = self.view(o_X + 8 * NT * 2, 4 * NT * 2, BF16, "p (k t) -> p k t", k=4); B_ypool = Buf("ypool")
        o_vn = o_X + 12 * NT * 2
        vn = self.view(o_vn, 18 * 512 * 2, BF16, "p (n c) -> p n c", n=18); B_vn = Buf("vn")
        cosT = self.view(o_vn, L * 4, F32); sinS = self.view(o_vn + L * 4, L * 4, F32); B_rope = Buf("rope")
        hT = self.view(o_H, NK * NT * 2, BF16, "p (k t) -> p k t", k=NK); B_h = Buf("hT")
        QT = self.view(o_R, 4 * NT * 2, BF16, "p (k t) -> p k t", k=4); B_q = Buf("QT")
        KT2 = self.view(o_R + 4 * NT * 2, 2 * NT * 2, BF16, "p (k t) -> p k t", k=2); B_k = Buf("KT2")
        VA = self.view(o_R + 6 * NT * 2, 18 * 256 * 2, BF16, "p (n h c) -> p n h c", n=18, h=2); B_v = Buf("VA")
        apad = [self.view(o_R + i * APLEN * 4, APLEN * 4, F32) for i in range(3)]
        B_ap = [Buf("apad%d" % i) for i in range(3)]
        plT = self.view(o_R + 3 * APLEN * 4, NT * 2, BF16); B_pl = Buf("plT")
        yT = self.view(o_R, NK * NT * 2, BF16, "p (k t) -> p k t", k=NK); B_y = Buf("yT")
        actT = self.view(o_R, NK * NT * 2, BF16, "p (k t) -> p k t", k=NK); B_a = Buf("actT")
        self.wslot = [self.view(o_W + s * CH * 2, CH * 2, BF16) for s in range(NSLOT)]
        self.wbuf = [Buf("w%d" % s) for s in range(NSLOT)]
        self.wsem = [self.dsem("s_w%d" % s) for s in range(NSLOT)]
        self.tmpb = [Buf("tmp%d" % i) for i in range(8)]; self.tmp_i = 0
        acc = self.view(o_acc, NT * 4, F32); B_acc = Buf("acc")
        vecT = self.view(o_vecT, NVEC * 4, F32); B_vec = Buf("vecT")
        ident = self.view(o_ident, 512, F32); B_c = Buf("consts")
        ones = self.view(o_ones, 512, F32)
        masks = self.view(o_masks, 512, BF16, "p (a q) -> p a q", a=2)
        identb = self.view(o_identb, 256, BF16)
        edge = self.view(o_edge, 512, F32, "p (g s d e) -> p g s d e", g=4, s=2, d=2)
        bsp = self.view(o_bsp, 2048, F32, "p (g q) -> p g q", g=4); B_bsp = Buf("bsp")
        small = self.view(o_small, 2048, F32); B_small = [Buf("small%d" % i) for i in range(8)]
        self.small_i = 0
        modv = self.view(o_mod, 2 * 48 * 4, F32, "p (t j) -> p t j", t=2); B_mod = Buf("mod")
        der = self.view(o_der, 2 * 6 * 8 * 4, F32, "p (t w k) -> p t w k", t=2, w=6); B_der = Buf("der")
        scT = self.view(o_sc, 32, BF16, "p (k t) -> p k t", k=8); B_sc = Buf("scT")
        esrow = self.view(o_es, 2048, BF16, "p (h c) -> p h c", h=2); B_es = Buf("esrow")
        Esel = self.view(o_E, 256, BF16)
        sk = self.view(o_sk, 64, F32); B_sk = Buf("sk")
        eps_t = small[:, 500:501]
        permM = self.view(o_perm, 512, F32)
        rsv = [self.view(o_rs + i * 2048, 2048, F32) for i in range(2)]; B_rs = [Buf('rs0'), Buf('rs1')]; self.rs_i = 0

        self.pst = [es.enter_context(nc.psum_tensor("ps%d" % i, [128, 512], F32)).ap() for i in range(8)]
        self.psb = [Buf("ps%d" % i) for i in range(8)]
        self.pspools = {"g": [0, 1, 2, 3, 4, 5, 6, 7], "f": [0, 1, 2, 3, 4, 5, 6], "s": [0, 1, 2, 3, 6, 7], "a": [4, 5], "o": [6, 7]}
        self.ps_i = {k: 0 for k in self.pspools}

        ds_c = self.dsem("s_const"); ds_x = [self.dsem("s_x%d" % i) for i in range(2)]
        ds_o = self.dsem("s_out"); ds_sp = self.dsem("s_spill"); ds_rl = [self.dsem("s_rl%d" % i) for i in range(NK)]

        self.ws_init(wst)
        self.ws_ensure(PREF)

        self.dma(sp, ident, d_ident, ds_c, writes=[B_c])
        self.dma(sp, permM, d_perm, ds_c, writes=[B_c])
        self.dma(sp, edge.rearrange("p g s d e -> p (g s d e)"), d_edge, ds_c, writes=[B_c])
        self.dma(pool, masks.rearrange("p a q -> p (a q)"), d_masks, ds_c, writes=[B_c])
        self.op(dve, lambda e: e.memset(ones, 1.0), writes=[B_c])
        self.op(dve, lambda e: e.tensor_copy(out=identb, in_=ident), reads=[B_c], writes=[B_c])
        self.op(dve, lambda e: e.memset(small, 0.0), writes=B_small)
        self.op(dve, lambda e: e.memset(eps_t, EPS), writes=B_small)
        self.op(dve, lambda e: e.memset(Esel[0:1, 0:64], 0.0), writes=[B_c])
        self.op(dve, lambda e: e.memset(Esel[0:1, 64:128], 1.0), writes=[B_c])
        for r in range(NVEC // 128):
            t_ap, t_b = self.tmp()
            self.dma(sp, t_ap[:, 0:128], vecs[r * 128:(r + 1) * 128, :], ds_c, writes=[t_b])
            ps, pb = self.psum()
            self.op(pe, lambda e: e.transpose(ps[:, 0:128], t_ap[:, 0:128], ident), reads=[t_b, B_c], writes=[pb])
            self.op(act, lambda e: e.copy(out=vecT[:, r * 128:(r + 1) * 128], in_=ps[:, 0:128]), reads=[pb], writes=[B_vec])
        self.op(act, lambda e: e.activation(out=scT[:, :, 0], in_=vecT[:, 288:296], func=AF.Silu), reads=[B_vec], writes=[B_sc])
        self.op(act, lambda e: e.activation(out=scT[:, :, 1], in_=vecT[:, 296:304], func=AF.Silu), reads=[B_vec], writes=[B_sc])

        if self.first:
            for n in range(NT // 128):
                t_ap = self.view(self.tmp_off + (n % 2) * 4096, 4096, F32); t_b = [self.tmpb[(n % 2) * 2], self.tmpb[(n % 2) * 2 + 1]]
                self.dma(sp, t_ap, xc[n * 128:(n + 1) * 128, :], ds_x[n % 2], writes=t_b)
                for hb in range(2):
                    ps, pb = self.psum()
                    for q in range(4):
                        kk = hb * 4 + q
                        self.op(pe, lambda e: e.transpose(ps[:, q * 128:(q + 1) * 128], t_ap[:, kk * 128:(kk + 1) * 128], ident),
                                reads=t_b + [B_c], writes=[pb], signal=(q == 3))
                    self.op(act if hb == 0 else dve,
                            (lambda e: e.copy(out=xT[:, hb * 4:hb * 4 + 4, n * 128:(n + 1) * 128], in_=ps.rearrange("p (a b) -> p a b", a=4))) if hb == 0 else
                            (lambda e: e.tensor_copy(out=xT[:, hb * 4:hb * 4 + 4, n * 128:(n + 1) * 128], in_=ps.rearrange("p (a b) -> p a b", a=4))),
                            reads=[pb], writes=[B_x])
            self.tmp_i = 4
        else:
            self.dma(sp, xT.rearrange("p k t -> p (k t)"), xs_in, ds_x[0], writes=[B_x])

        def smallv(n=1):
            i = self.small_i; self.small_i = (i + 1) % 8
            return small[:, i * 8:i * 8 + n], B_small[i]

        def stats_rstd(src, src_b, t0, n):
            ps, pb = self.psum()
            for kk in range(NK):
                sq, sqb = self.tmp()
                self.op(act, lambda e: e.activation(out=sq[:, 0:n], in_=src[:, kk, t0:t0 + n], func=AF.Square), reads=[src_b], writes=[sqb])
                self.op(pe, lambda e: e.matmul(ps[:, 0:n], lhsT=ones, rhs=sq[:, 0:n], start=(kk == 0), stop=(kk == NK - 1)),
                        reads=[sqb, B_c], writes=[pb], signal=(kk == NK - 1))
            r1, r1b = self.tmp()
            self.op(act, lambda e: e.activation(out=r1[:, 0:n], in_=ps[:, 0:n], func=AF.Sqrt, bias=eps_t, scale=1.0 / D), reads=[pb] + B_small, writes=[r1b])
            rs, rsb = rsv[self.rs_i], B_rs[self.rs_i]; self.rs_i ^= 1
            self.op(dve, lambda e: e.reciprocal(out=rs[:, 0:n], in_=r1[:, 0:n]), reads=[r1b], writes=[rsb])
            return rs, rsb

        def norm_mod(wG, wS):
            for (t0, n) in TT:
                ps, pb = self.psum()
                for kk in range(NK):
                    sq, sqb = self.tmp()
                    if kk % 4 == 3:
                        self.op(pool, lambda e: e.tensor_tensor(out=sq[:, 0:n], in0=xT[:, kk, t0:t0 + n], in1=xT[:, kk, t0:t0 + n], op=ALU.mult), reads=[B_x, B_xk[kk]], writes=[sqb])
                    else:
                        self.op(act, lambda e: e.activation(out=sq[:, 0:n], in_=xT[:, kk, t0:t0 + n], func=AF.Square), reads=[B_x, B_xk[kk]], writes=[sqb])
                    self.op(pe, lambda e: e.matmul(ps[:, 0:n], lhsT=ones, rhs=sq[:, 0:n], start=(kk == 0), stop=(kk == NK - 1)),
                            reads=[sqb, B_c], writes=[pb], signal=(kk == NK - 1))
                r1, r1b = self.tmp()
                self.op(act, lambda e: e.activation(out=r1[:, 0:n], in_=ps[:, 0:n], func=AF.Sqrt, bias=eps_t, scale=1.0 / D), reads=[pb] + B_small, writes=[r1b])
                self.op(dve, lambda e: e.reciprocal(out=acc[:, t0:t0 + n], in_=r1[:, 0:n]), reads=[r1b], writes=[B_acc])
            for (t0, n) in TT:
                ti = 0 if t0 < L else 1
                for kk in range(NK):
                    tm, tmb = self.tmp()
                    self.op(dve, lambda e: e.tensor_tensor(out=tm[:, 0:n], in0=xT[:, kk, t0:t0 + n], in1=acc[:, t0:t0 + n], op=ALU.mult),
                            reads=[B_x, B_xk[kk], B_acc], writes=[tmb])
                    if kk % 4 == 3:
                        self.op(pool, lambda e: e.tensor_scalar(out=hT[:, kk, t0:t0 + n], in0=tm[:, 0:n], scalar1=der[:, ti, wG, kk:kk + 1], scalar2=der[:, ti, wS, kk:kk + 1],
                                                                op0=ALU.mult, op1=ALU.add), reads=[tmb, B_der], writes=[B_h])
                    else:
                        self.op(act, lambda e: e.activation(out=hT[:, kk, t0:t0 + n], in_=tm[:, 0:n], func=AF.Identity,
                                                            bias=der[:, ti, wS, kk:kk + 1], scale=der[:, ti, wG, kk:kk + 1]),
                                reads=[tmb, B_der], writes=[B_h])

        def proj_fm(w, wb, src, src_b, nk, t0, n, pool="g"):
            ps, pb = self.psum(pool)
            for kk in range(nk):
                self.op(pe, lambda e: e.matmul(ps[:, 0:n], lhsT=w[:, kk, :], rhs=src[:, kk, t0:t0 + n], start=(kk == 0), stop=(kk == nk - 1)),
                        reads=[wb, src_b], writes=[pb], signal=(kk == nk - 1))
            return ps, pb

        def rope_evac(ps, pb, dst, dst_b, t0, n):
            t1, t1b = self.tmp(); t2, t2b = self.tmp()
            self.op(dve, lambda e: e.tensor_tensor(out=t1[:, 0:n], in0=ps[:, 0:n], in1=cosT[:, t0:t0 + n], op=ALU.mult), reads=[pb, B_rope], writes=[t1b])
            for (o, i) in ((0, 32), (32, 0), (64, 96), (96, 64)):
                self.op(dve, lambda e: e.tensor_tensor(out=t2[o:o + 32, 0:n], in0=ps[i:i + 32, 0:n], in1=sinS[i:i + 32, t0:t0 + n], op=ALU.mult),
                        reads=[pb, B_rope], writes=[t2b])
            self.op(pool, lambda e: e.tensor_tensor(out=dst, in0=t1[:, 0:n], in1=t2[:, 0:n], op=ALU.add), reads=[t1b, t2b], writes=[dst_b])

        def stop(tag):
            if self.stop_at == tag:
                raise StopBuild()
        self.stop = stop
        try:
            stop('load')
            for li in range(self.nl):
                vb = 72 * li
                self.dma(sp, bsp.rearrange("p g q -> p (g q)"), d_bsp[:, li * 512:(li + 1) * 512], ds_c, writes=[B_bsp])
                self.dma(sp, sk[0:1, 0:8], d_sink[0:1, li * 8:(li + 1) * 8], ds_c, writes=[B_sk])
                psm, psmb = self.pst[7], self.psb[7]
                psv = psm[:, 0:96].rearrange("p (t j) -> p t j", t=2)

                def mod_tile():
                    w, wb = self.wnext("mod")
                    j = self.mod_j; self.mod_j = (j + 1) % 48
                    for kk in range(NK):
                        self.op(pe, lambda e: e.matmul(psv[:, :, j], lhsT=w[:, kk, :], rhs=scT[:, kk, :], start=(kk == 0), stop=(kk == NK - 1)),
                                reads=[wb, B_sc], writes=[psmb], signal=(kk == NK - 1))
                if li == 0:
                    self.mod_j = 0
                    for j in range(48): mod_tile()
                for ti in range(2):
                    self.op(dve, lambda e: e.tensor_tensor(out=modv[:, ti, :], in0=psv[:, ti, :], in1=vecT[:, vb:vb + 48], op=ALU.add),
                            reads=[psmb, B_vec], writes=[B_mod])
                for ti in range(2):
                    m6 = modv[:, ti, :].rearrange("p (w k) -> p w k", w=6)
                    self.op(dve, lambda e: e.scalar_tensor_tensor(out=der[:, ti, 0, :], in0=m6[:, 1, :], scalar=1.0, in1=vecT[:, vb + 48:vb + 56], op0=ALU.add, op1=ALU.mult),
                            reads=[B_mod, B_vec], writes=[B_der])
                    self.op(dve, lambda e: e.scalar_tensor_tensor(out=der[:, ti, 3, :], in0=m6[:, 4, :], scalar=1.0, in1=vecT[:, vb + 56:vb + 64], op0=ALU.add, op1=ALU.mult),
                            reads=[B_mod, B_vec], writes=[B_der])
                    for (wd, ws_) in ((1, 0), (2, 2), (4, 3), (5, 5)):
                        self.op(dve, lambda e: e.tensor_copy(out=der[:, ti, wd, :], in_=m6[:, ws_, :]), reads=[B_mod], writes=[B_der])
                self.op(act, lambda e: e.activation(out=esrow[0:1].rearrange("p h (m q) -> p (h m) q", m=4),
                                                    in_=sk[0:1, 0:8].unsqueeze(2).broadcast_to([1, 8, 128]), func=AF.Exp),
                        reads=[B_sk], writes=[B_es])

                stop('mod')
                norm_mod(0, 1)
                stop('norm1')
                import os
                if os.environ.get("KSPB"): self.barrier()
                self.dma(sp, xspill, xT.rearrange("p k t -> p (k t)"), ds_sp, reads=[B_x] + B_xk)
                if os.environ.get("KSPB"):
                    sp.e.wait_ge(ds_sp.sem, ds_sp.val); sp.seen[ds_sp.sem.num] = ds_sp.val
                    self.barrier()
                for b_ in (B_rope, B_yattn, B_ysg, B_ypool, B_vn):
                    b_.r.update(B_x.r)
                    for bk in B_xk: b_.r.update(bk.r)
                self.dma(sp, cosT, d_cos, ds_c, writes=[B_rope])
                self.dma(sp, sinS, d_sin, ds_c, writes=[B_rope])
                self.op(pool, lambda e: e.memset(VA[:, :, :, 64:128], 1.0), writes=[B_v])

                w, wb = self.wnext("v")
                for n0 in range(0, 18, 4):
                    nb = min(4, 18 - n0)
                    ps, pb = self.psum()
                    for i in range(nb):
                        n = n0 + i
                        for kk in range(NK):
                            self.op(pe, lambda e: e.matmul(ps[:, i * 128:(i + 1) * 128], lhsT=hT[:, kk, n * 128:(n + 1) * 128], rhs=w[:, kk, :],
                                                           start=(kk == 0), stop=(kk == NK - 1)),
                                    reads=[wb, B_h], writes=[pb], signal=(kk == NK - 1 and i == nb - 1))
                    self.op(act, lambda e: e.copy(out=VA[:, n0:n0 + nb, :, 0:64], in_=ps[:, 0:nb * 128].rearrange("p (n h c) -> p n h c", n=nb, h=2)),
                            reads=[pb], writes=[B_v])

                for hk in range(2):
                    w, wb = self.wnext("k")
                    for (t0, n) in TT:
                        ps, pb = proj_fm(w, wb, hT, B_h, NK, t0, n)
                        if t0 < L: rope_evac(ps, pb, KT2[:, hk, t0:t0 + n], B_k, t0, n)
                        else: self.op(act, lambda e: e.copy(out=KT2[:, hk, t0:t0 + n], in_=ps[:, 0:n]), reads=[pb], writes=[B_k])
                for j in range(4):
                    w, wb = self.wnext("q")
                    for (t0, n) in TT:
                        ps, pb = proj_fm(w, wb, hT, B_h, NK, t0, n)
                        if t0 < L: rope_evac(ps, pb, QT[:, j, t0:t0 + n], B_q, t0, n)
                        else: self.op(act, lambda e: e.copy(out=QT[:, j, t0:t0 + n], in_=ps[:, 0:n]), reads=[pb], writes=[B_q])
                stop('qkv')
                blocks = []
                for i in range(16):
                    kbs = []
                    for j in (i - 1, i, i + 1):
                        if 0 <= j < 16:
                            kbs.append((j * 128, j, None if j == i else (0 if j < i else 1)))
                    kbs += [(L, 16, None), (L + 128, 17, None)]
                    for hk in range(2):
                        blocks.append((i * 128, kbs, hk))
                for ic in range(2):
                    for hk in range(2):
                        blocks.append((L + ic * 128, [(L, 16, None), (L + 128, 17, None)], hk))
                jobs = []
                for bid, (qt0, kbs, hk) in enumerate(blocks):
                    for bi, (kt0, n_kb, mk) in enumerate(kbs):
                        jobs.append(dict(bid=bid, qt0=qt0, hk=hk, bi=bi, nkb=len(kbs), kt0=kt0, n_kb=n_kb, mk=mk))
                accs = {}

                def emit_qk(jb):
                    hk = jb["hk"]; kt0 = jb["kt0"]; qt0 = jb["qt0"]
                    sA, sAb = self.psum("s"); sB, sBb = self.psum("s")
                    sbank = ((sA, sAb), (sB, sBb)); jb["sbank"] = sbank
                    for m in range(4):
                        h = 4 * hk + m; j2 = h // 2; e2 = h % 2
                        sp_, spb_ = sbank[e2]
                        self.op(pe, lambda e: e.matmul(sp_[:, (m // 2) * 128:(m // 2 + 1) * 128], lhsT=KT2[64 * e2:64 * e2 + 64, hk, kt0:kt0 + 128],
                                                       rhs=QT[64 * e2:64 * e2 + 64, j2, qt0:qt0 + 128], start=True, stop=True),
                                reads=[B_k, B_q], writes=[spb_], signal=(m >= 2))

                def emit_rest(jb):
                    hk = jb["hk"]; bi = jb["bi"]; mk = jb["mk"]; qt0 = jb["qt0"]
                    if bi == 0:
                        accs[jb["bid"]] = self.psum("a")
                    accp, accb = accs[jb["bid"]]
                    pt, ptb = self.tmp(BF16)
                    pt4 = pt.rearrange("p (a b q) -> p a b q", a=2, b=2)
                    for e2 in range(2):
                        sp_, spb_ = jb["sbank"][e2]
                        self.op(act, lambda e: e.activation(out=pt4[:, :, e2, :], in_=sp_[:, 0:256].rearrange("p (a q) -> p a q", a=2), func=AF.Exp, scale=0.125),
                                reads=[spb_], writes=[ptb])
                    if mk is not None:
                        ptv = pt.rearrange("p (m q) -> p m q", m=4)
                        self.op(pool, lambda e: e.tensor_tensor(out=ptv, in0=ptv, in1=masks[:, mk, :].unsqueeze(1).broadcast_to([128, 4, 128]), op=ALU.mult),
                                reads=[ptb, B_c], writes=[ptb])
                    self.op(pe, lambda e: e.matmul(accp, lhsT=VA[:, jb["n_kb"], hk, :], rhs=pt, start=(bi == 0), stop=False),
                            reads=[B_v, ptb], writes=[accb], signal=False)
                    if bi == jb["nkb"] - 1:
                        self.op(pe, lambda e: e.matmul(accp, lhsT=Esel[0:1, :], rhs=esrow[0:1, hk, :], start=False, stop=True),
                                reads=[B_c, B_es], writes=[accb], signal=True)
                        rd, rdb = self.tmp()
                        self.op(dve, lambda e: e.reciprocal(out=rd[0:64, :], in_=accp[64:128, :]), reads=[accb], writes=[rdb])
                        av = accp[0:64, :].rearrange("p (a b q) -> p a b q", a=2, b=2)
                        rv = rd[0:64, :].rearrange("p (a b q) -> p a b q", a=2, b=2)
                        for par in range(2):
                            self.op(dve, lambda e: e.tensor_tensor(out=yattn[64 * par:64 * par + 64, 2 * hk:2 * hk + 2, qt0:qt0 + 128],
                                                                   in0=av[:, :, par, :], in1=rv[:, :, par, :], op=ALU.mult),
                                    reads=[accb, rdb], writes=[B_yattn])
                LA = 2
                for idx in range(len(jobs) + LA):
                    if idx < len(jobs): emit_qk(jobs[idx])
                    if idx - LA >= 0: emit_rest(jobs[idx - LA])
                self.barrier()

                stop('attn')
                wsv = [self.wnext("sv") for _ in range(2)]
                for n in range(18):
                    ps, pb = self.psum()
                    for h2 in range(2):
                        w, wb = wsv[h2]
                        for kk in range(NK):
                            self.op(pe, lambda e: e.matmul(ps[:, h2 * 256:(h2 + 1) * 256], lhsT=hT[:, kk, n * 128:(n + 1) * 128], rhs=w[:, kk, :],
                                                           start=(kk == 0), stop=(kk == NK - 1)),
                                    reads=[wb, B_h], writes=[pb], signal=(kk == NK - 1 and h2 == 1))
                    gv, gvb = self.tmp()
                    self.op(act, lambda e: e.activation(out=gv, in_=ps, func=AF.Gelu_apprx_tanh), reads=[pb], writes=[gvb])
                    jk, jkb = self.tmp()
                    ss, ssb = smallv()
                    self.op(act, lambda e: e.activation(out=jk, in_=gv, func=AF.Square, accum_out=ss), reads=[gvb], writes=[jkb, ssb])
                    s2, s2b = smallv()
                    self.op(act, lambda e: e.activation(out=s2, in_=ss, func=AF.Sqrt, bias=eps_t, scale=1.0 / 512), reads=[ssb] + B_small[7:8], writes=[s2b])
                    s3, s3b = smallv()
                    self.op(dve, lambda e: e.reciprocal(out=s3, in_=s2), reads=[s2b], writes=[s3b])
                    self.op(dve, lambda e: e.tensor_scalar(out=vn[:, n, :], in0=gv, scalar1=s3, scalar2=None, op0=ALU.mult), reads=[gvb, s3b], writes=[B_vn])
                for g in range(4):
                    w, wb = self.wnext("u")
                    for (t0, n) in TT:
                        ps, pb = proj_fm(w, wb, hT, B_h, NK, t0, n)
                        self.op(act, lambda e: e.activation(out=ysg[:, g, t0:t0 + n], in_=ps[:, 0:n], func=AF.Gelu_apprx_tanh), reads=[pb], writes=[B_ysg])
                wws = [self.wnext("ws") for _ in range(4)]
                for g in range(4):
                    w, wb = wws[g]
                    for (t0, n) in TT:
                        nb = n // 128; c0 = t0 // 128
                        ps, pb = self.psum()
                        for i in range(nb):
                            self.op(pe, lambda e: e.matmul(ps[:, i * 128:(i + 1) * 128], lhsT=vn[:, c0 + i, g * 128:(g + 1) * 128], rhs=w[:, 0, :], start=True, stop=True),
                                    reads=[wb, B_vn], writes=[pb], signal=(i == nb - 1))
                        tm, tmb = self.tmp()
                        self.op(dve, lambda e: e.scalar_tensor_tensor(out=tm[:, 0:n].rearrange("p (i q) -> p i q", i=nb),
                                                                      in0=ps[:, 0:n].rearrange("p (i q) -> p i q", i=nb),
                                                                      scalar=vecT[:, vb + 68 + g:vb + 69 + g],
                                                                      in1=bsp[:, g, :].unsqueeze(1).broadcast_to([128, nb, 128]), op0=ALU.mult, op1=ALU.add),
                                reads=[pb, B_vec, B_bsp], writes=[tmb])
                        self.op(pool, lambda e: e.tensor_tensor(out=ysg[:, g, t0:t0 + n], in0=ysg[:, g, t0:t0 + n], in1=tm[:, 0:n], op=ALU.mult),
                                reads=[tmb, B_ysg], writes=[B_ysg])
                self.barrier()

                stop('sg')
                for i3 in range(3):
                    self.op(pool, lambda e: e.memset(apad[i3], 0.0), writes=[B_ap[i3]])
                A = apad[0]

                def seqpos(t0):
                    return XPAD + t0 if t0 < L else CTX_OFF + (t0 - L)
                for g in range(4):
                    wd = POOLW[g]; hw = wd // 2
                    w, wb = self.wnext("pc")
                    for (t0, n) in TT:
                        ps, pb = proj_fm(w, wb, hT, B_h, NK, t0, n)
                        p0 = seqpos(t0)
                        self.op(act, lambda e: e.copy(out=A[:, p0:p0 + n], in_=ps[:, 0:n]), reads=[pb], writes=[B_ap[0]])
                    cur, curb = A, B_ap[0]; step = 1; idx = 1
                    while step < wd:
                        nxt, nxtb = apad[idx], B_ap[idx]
                        ln = APLEN - step
                        self.op(dve, lambda e: e.tensor_tensor(out=nxt[:, 0:ln], in0=cur[:, 0:ln], in1=cur[:, step:step + ln], op=ALU.add),
                                reads=[curb], writes=[nxtb])
                        cur, curb = nxt, nxtb; step *= 2; idx = 3 - idx
                    for si, (s0, Ls, tok0) in enumerate(((XPAD, L, 0), (CTX_OFF, C, L))):
                        for c0 in range(0, Ls, 1024):
                            ln = min(1024, Ls - c0)
                            self.op(dve, lambda e: e.scalar_tensor_tensor(out=plT[:, tok0 + c0:tok0 + c0 + ln], in0=cur[:, s0 + c0 - hw:s0 + c0 - hw + ln], scalar=1.0 / wd,
                                                                          in1=A[:, s0 + c0:s0 + c0 + ln], op0=ALU.mult, op1=ALU.subtract),
                                    reads=[curb, B_ap[0]], writes=[B_pl])
                        for side, (e0, ne) in enumerate(((0, hw), (Ls - hw + 1, hw - 1))):
                            if ne <= 0: continue
                            sv_, svb = smallv(8)
                            self.op(dve, lambda e: e.tensor_tensor(out=sv_[:, 0:ne], in0=cur[:, s0 + e0 - hw:s0 + e0 - hw + ne], in1=edge[:, g, si, side, 0:ne], op=ALU.mult),
                                    reads=[curb, B_c], writes=[svb])
                            self.op(dve, lambda e: e.tensor_tensor(out=plT[:, tok0 + e0:tok0 + e0 + ne], in0=sv_[:, 0:ne], in1=A[:, s0 + e0:s0 + e0 + ne], op=ALU.subtract),
                                    reads=[svb, B_ap[0]], writes=[B_pl])
                    w2, w2b = self.wnext("wp")
                    for (t0, n) in TT:
                        ps, pb = self.psum()
                        self.op(pe, lambda e: e.matmul(ps[:, 0:n], lhsT=w2[:, 0, :], rhs=plT[:, t0:t0 + n], start=True, stop=True), reads=[w2b, B_pl], writes=[pb])
                        self.op(act, lambda e: e.activation(out=ypool[:, g, t0:t0 + n], in_=ps[:, 0:n], func=AF.Identity, scale=vecT[:, vb + 64 + g:vb + 65 + g]),
                                reads=[pb, B_vec], writes=[B_ypool])
                self.barrier()

                stop('pool')
                ysrc = [(ypool, B_ypool), (yattn, B_yattn), (ysg, B_ysg)]
                for j in range(8):
                    for b in range(3):
                        wg, wgb = self.wnext("gate")
                        wr, wrb = self.wnext("br")
                        for (t0, n) in TT:
                            psg, pgb = proj_fm(wg, wgb, hT, B_h, NK, t0, n)
                            sg, sgb = self.tmp()
                            self.op(act, lambda e: e.activation(out=sg[:, 0:n], in_=psg[:, 0:n], func=AF.Sigmoid), reads=[pgb], writes=[sgb])
                            psb_, pbb = proj_fm(wr, wrb, ysrc[b][0], ysrc[b][1], 4, t0, n)
                            if b == 0:
                                self.op(dve, lambda e: e.tensor_tensor(out=acc[:, t0:t0 + n], in0=psb_[:, 0:n], in1=sg[:, 0:n], op=ALU.mult), reads=[pbb, sgb], writes=[B_acc])
                            else:
                                tm, tmb = self.tmp()
                                self.op(dve, lambda e: e.tensor_tensor(out=tm[:, 0:n], in0=psb_[:, 0:n], in1=sg[:, 0:n], op=ALU.mult), reads=[pbb, sgb], writes=[tmb])
                                if b == 1:
                                    self.op(pool, lambda e: e.tensor_tensor(out=acc[:, t0:t0 + n], in0=acc[:, t0:t0 + n], in1=tm[:, 0:n], op=ALU.add), reads=[tmb, B_acc], writes=[B_acc])
                                else:
                                    self.op(pool, lambda e: e.tensor_tensor(out=yT[:, j, t0:t0 + n], in0=acc[:, t0:t0 + n], in1=tm[:, 0:n], op=ALU.add), reads=[tmb, B_acc], writes=[B_y])
                xsp3 = xspill.rearrange("p (k t) -> p k t", k=NK)
                al_ = ([B_yattn], [B_yattn], [B_ysg], [B_ysg], [B_ypool], [B_ypool], [B_vn, B_rope], [B_vn, B_rope])
                for kk in range(NK):
                    self.dma(sp, xT[:, kk, :], xsp3[:, kk, :], ds_rl[kk], writes=[B_xk[kk]] + al_[kk])

                stop('merge')
                for j in range(8):
                    w, wb = self.wnext("out")
                    for (t0, n) in TT:
                        ti = 0 if t0 < L else 1
                        ps, pb = proj_fm(w, wb, yT, B_y, NK, t0, n)
                        self.op(dve, lambda e: e.scalar_tensor_tensor(out=xT[:, j, t0:t0 + n], in0=ps[:, 0:n], scalar=der[:, ti, 2, j:j + 1], in1=xT[:, j, t0:t0 + n],
                                                                      op0=ALU.mult, op1=ALU.add), reads=[pb, B_der, B_xk[j]], writes=[B_x, B_xk[j]])
                stop('outproj')
                norm_mod(3, 4)
                self.barrier()
                nmod = 0
                for gi, grp in enumerate(FFG):
                    for fi, f in enumerate(grp):
                        wg, wgb = self.wnext("fg")
                        wu, wub = self.wnext("fu")
                        for (t0, n) in TT:
                            psg, pgb = proj_fm(wg, wgb, hT, B_h, NK, t0, n, pool="f")
                            sg, sgb = self.tmp()
                            self.op(act, lambda e: e.activation(out=sg[:, 0:n], in_=psg[:, 0:n], func=AF.Silu), reads=[pgb], writes=[sgb])
                            psu, pub = proj_fm(wu, wub, hT, B_h, NK, t0, n, pool="f")
                            self.op(dve, lambda e: e.tensor_tensor(out=actT[:, fi, t0:t0 + n], in0=psu[:, 0:n], in1=sg[:, 0:n], op=ALU.mult), reads=[pub, sgb], writes=[B_a])
                        if li < self.nl - 1:
                            for _ in range(MOD_PER_F):
                                if nmod < 48:
                                    mod_tile(); nmod += 1
                    for j in range(8):
                        w, wb = self.wnext("fo")
                        for (t0, n) in TT:
                            ti = 0 if t0 < L else 1
                            ps, pb = proj_fm(w, wb, actT, B_a, len(grp), t0, n, pool="f")
                            self.op(dve, lambda e: e.scalar_tensor_tensor(out=xT[:, j, t0:t0 + n], in0=ps[:, 0:n], scalar=der[:, ti, 5, j:j + 1], in1=xT[:, j, t0:t0 + n],
                                                                          op0=ALU.mult, op1=ALU.add), reads=[pb, B_der, B_x], writes=[B_x])
                self.barrier()

        except StopBuild:
            pass

        if self.last and self.stop_at is None:
            fgc = 304
            for (t0, n) in TT[:4]:
                rs, rsb = stats_rstd(xT, B_x, t0, n)
                for kk in range(NK):
                    self.op(dve, lambda e: e.scalar_tensor_tensor(out=xT[:, kk, t0:t0 + n], in0=xT[:, kk, t0:t0 + n], scalar=vecT[:, fgc + kk:fgc + kk + 1], in1=rs[:, 0:n],
                                                                  op0=ALU.mult, op1=ALU.mult), reads=[B_x, B_vec, rsb], writes=[B_x])
            self.barrier()
            ob = [Buf("o0"), Buf("o1")]
            for n in range(L // 128):
                o_ap = self.view(o_H + (n % 2) * 4096, 4096, F32)
                for hb in range(2):
                    ps, pb = self.psum()
                    for q in range(4):
                        kk = hb * 4 + q
                        self.op(pe, lambda e: e.transpose(ps[:, q * 128:(q + 1) * 128], xT[:, kk, n * 128:(n + 1) * 128], ident),
                                reads=[B_x, B_c], writes=[pb], signal=(q == 3))
                    if hb == 0:
                        self.op(act, lambda e: e.copy(out=o_ap[:, 0:512], in_=ps), reads=[pb], writes=[ob[n % 2]])
                    else:
                        self.op(dve, lambda e: e.tensor_copy(out=o_ap[:, 512:1024], in_=ps), reads=[pb], writes=[ob[n % 2]])
                self.dma(sp, out[n * 128:(n + 1) * 128, :], o_ap, ds_o, reads=[ob[n % 2]])
        else:
            self.barrier()
            if self.dump == 'X':
                self.dma(sp, xs_out, xT.rearrange("p k t -> p (k t)"), ds_o, reads=[B_x])
            else:
                o_d = o_H if self.dump == 'H' else o_R
                self.dma(sp, xs_out[:, 0:NK * NT // 2], self.view(o_d, NK * NT * 2, F32), ds_o, reads=[B_h, B_y])
        sp.e.wait_ge(ds_o.sem, ds_o.val)
        self.barrier()
        es.close()
        return nc


_CACHE = {}


def _prog(first, last, nl):
    key = (first, last, nl)
    if key not in _CACHE:
        _CACHE[key] = Prog(first, last, nl).build()
    return _CACHE[key]


def _vecs(W, b):
    v = np.zeros((NVEC, 128), np.float32)
    for l in range(DEPTH):
        base = 72 * l
        v[base:base + 48] = W["b_mod"][l].reshape(48, 128)
        v[base + 48:base + 56] = W["norm1_gain"][l].reshape(8, 128)
        v[base + 56:base + 64] = W["norm2_gain"][l].reshape(8, 128)
        v[base + 64:base + 68] = W["pool_scale"][l].reshape(4, 128)
        v[base + 68:base + 72] = W["sg_v_gain"][l].reshape(4, 128)
    v[288:296] = W["c"][b].reshape(8, 128)
    v[296:304] = W["c_ctx"].reshape(8, 128)
    v[304:312] = W["final_gain"].reshape(8, 128)
    return v


def kernel(**inputs):
    W = {k: np.asarray(v) for k, v in inputs.items()}
    consts = const_tables()
    ncores = 8
    plan = build_plan(DEPTH)
    wst = np.concatenate([pack_layer(l, W, plan) for l in range(DEPTH)], axis=0)
    bsp = np.concatenate([np.broadcast_to(W["b_spatial"][l].reshape(1, 512), (128, 512)) for l in range(DEPTH)], axis=1).astype(np.float32)
    sink = np.concatenate([W["attn_sink"][l].reshape(1, 8) for l in range(DEPTH)], axis=1).astype(np.float32)
    nc = _prog(True, True, DEPTH)
    in_maps = []
    for b in range(ncores):
        in_maps.append(dict(wst=wst, vecs=_vecs(W, b), ident=consts["ident"], permM=consts["permM"], cosT=consts["cosT"], sinS=consts["sinS"], masks=consts["masks"],
                            edge=consts["edge"], bsp=np.ascontiguousarray(bsp), sink=sink,
                            xc=np.ascontiguousarray(np.concatenate([W["x"][b], W["ctx"][b]], axis=0))))
    res = run_bass_kernel_spmd(nc, in_maps, core_ids=list(range(ncores)))
    out = np.stack([res.results[b]["out"] for b in range(ncores)], axis=0)
    return out.astype(np.float32)
```
